# Optimizing a Trainium2 kernel written in Bass

```python
import jax
import jax.numpy as jnp
from jax import lax
import numpy as np

D_MODEL = 1024
BATCH = 16
SEQ = 4096
DEPTH = 1

N_MEM = 256
EPS = 1e-6
D_FF = 2816
HG_HEADS = 4
HG_DK = 128
HG_DV = 128
HG_CHUNK = 64
HG_K = HG_HEADS * HG_DK
HG_V = HG_HEADS * HG_DV
HG_COLS = 2 * HG_K + 2 * HG_V
DIL_GROUPS = ((128, 1), (512, 4), (2048, 16))
DIL_HEADS = 4
DIL_DH = 128
DIL_W = DIL_HEADS * DIL_DH
DIL_COLS = len(DIL_GROUPS) * 3 * DIL_W
ALIBI_HEADS = len(DIL_GROUPS) * DIL_HEADS
MEM_HEADS = 4
MEM_DH = 128
MEM_W = MEM_HEADS * MEM_DH
N_BRANCH = 3
GATE_COLS = N_BRANCH * D_MODEL
SPLITS = (HG_K, 2 * HG_K, 2 * HG_K + HG_V, HG_COLS, HG_COLS + DIL_COLS, HG_COLS + DIL_COLS + MEM_W)
D_IN = HG_COLS + DIL_COLS + MEM_W + GATE_COLS

kernel_name = 'hybrid_hgrn2_dilated_alibi_macaron'


def rmsnorm(x, w):
    xf = x.astype(jnp.float32)
    y = xf * lax.rsqrt(jnp.mean(xf * xf, axis=-1, keepdims=True) + EPS)
    return (y * w.astype(jnp.float32)).astype(x.dtype)


def swiglu(x, w_gu, w_down):
    g, u = jnp.split(x @ w_gu, 2, axis=-1)
    return (jax.nn.silu(g) * u) @ w_down


def alibi_slopes():
    return 2.0 ** (-8.0 * jnp.arange(1, ALIBI_HEADS + 1, dtype=jnp.float32) / ALIBI_HEADS)


def hgrn2(q_raw, f_raw, i_raw, lb):
    B, S = q_raw.shape[:2]
    C = HG_CHUNK
    N = S // C
    f32 = jnp.float32
    q = jax.nn.silu(q_raw.astype(f32)).reshape(B, N, C, HG_HEADS, HG_DK)
    lbh = lb.reshape(HG_HEADS, HG_DK)
    f = lbh + (1.0 - lbh) * jax.nn.sigmoid(f_raw.astype(f32).reshape(B, N, C, HG_HEADS, HG_DK))
    k = 1.0 - f
    v = i_raw.astype(f32).reshape(B, N, C, HG_HEADS, HG_DV)
    b = jnp.cumsum(jnp.log(f), axis=2)
    ref = b[:, :, C // 2 - 1:C // 2]
    scores = jnp.einsum('bnchk,bnshk->bnhcs', q * jnp.exp(b - ref), k * jnp.exp(ref - b))
    causal = jnp.tril(jnp.ones((C, C), dtype=bool))
    scores = jnp.where(causal, scores, 0.0)
    o_intra = jnp.einsum('bnhcs,bnshv->bnchv', scores, v)
    b_last = b[:, :, -1:]
    kv = jnp.einsum('bnshk,bnshv->nbhkv', k * jnp.exp(b_last - b), v)
    decay = jnp.moveaxis(jnp.exp(b_last[:, :, 0]), 1, 0)

    def step(state, inp):
        kv_n, dec_n = inp
        return dec_n[..., None] * state + kv_n, state

    s0 = jnp.zeros((B, HG_HEADS, HG_DK, HG_DV), f32)
    _, s_prev = lax.scan(step, s0, (kv, decay))
    o_inter = jnp.einsum('bnchk,nbhkv->bnchv', q * jnp.exp(b), s_prev)
    return (o_intra + o_inter).reshape(B, S, HG_HEADS, HG_DV)


def dilated_group(q, k, v, window, dil, slopes):
    B, S, H, dh = q.shape
    nk = window // dil
    span = nk * dil
    Sp = -(-S // span) * span
    L = Sp // dil
    nb = L // nk

    def to_sub(t):
        t = jnp.pad(t, ((0, 0), (0, Sp - S), (0, 0), (0, 0)))
        return t.reshape(B, L, dil, H, dh).transpose(0, 2, 1, 3, 4).reshape(B * dil, nb, nk, H, dh)

    qs, ks, vs = to_sub(q), to_sub(k), to_sub(v)
    kb = jnp.concatenate([jnp.pad(ks, ((0, 0), (1, 0), (0, 0), (0, 0), (0, 0)))[:, :-1], ks], axis=2)
    vb = jnp.concatenate([jnp.pad(vs, ((0, 0), (1, 0), (0, 0), (0, 0), (0, 0)))[:, :-1], vs], axis=2)
    s = jnp.einsum('znqhd,znkhd->znhqk', qs, kb).astype(jnp.float32) * (dh ** -0.5)
    qi = jnp.arange(nk)[:, None]
    kj = jnp.arange(2 * nk)[None, :]
    delta = nk + qi - kj
    blk = jnp.arange(nb)[:, None, None]
    valid = (delta >= 0) & (delta <= nk) & ((blk > 0) | (kj >= nk))
    s = s - slopes[:, None, None] * (delta * dil).astype(jnp.float32)
    s = jnp.where(valid[:, None], s, -jnp.inf)
    m = jnp.max(s, axis=-1)
    p = jnp.exp(s - m[..., None])
    l = jnp.sum(p, axis=-1)
    o = jnp.einsum('znhqk,znkhd->znqhd', p.astype(vb.dtype), vb).astype(jnp.float32)
    o = o.reshape(B, dil, L, H, dh).transpose(0, 2, 1, 3, 4).reshape(B, Sp, H, dh)[:, :S]

    def stat_back(t):
        return t.transpose(0, 1, 3, 2).reshape(B, dil, L, H).transpose(0, 2, 1, 3).reshape(B, Sp, H)[:, :S]

    return o, stat_back(m), stat_back(l)


def dilated_attention(dil_cols):
    B, S = dil_cols.shape[:2]
    qkv = dil_cols.reshape(B, S, len(DIL_GROUPS), 3, DIL_HEADS, DIL_DH)
    slopes = alibi_slopes()
    outs, maxs, sums = [], [], []
    for g, (window, dil) in enumerate(DIL_GROUPS):
        o, m, l = dilated_group(qkv[:, :, g, 0], qkv[:, :, g, 1], qkv[:, :, g, 2], window, dil,
                                slopes[g * DIL_HEADS:(g + 1) * DIL_HEADS])
        outs.append(o)
        maxs.append(m)
        sums.append(l)
    m_all = jnp.stack(maxs)
    w = jnp.exp(m_all - jnp.max(m_all, axis=0))
    num = jnp.sum(w[..., None] * jnp.stack(outs), axis=0)
    den = jnp.sum(w * jnp.stack(sums), axis=0)
    return (num / den[..., None]).reshape(B, S, DIL_W).astype(dil_cols.dtype)


def memory_attention(mq, mem, mem_norm_w, w_mem_kv):
    B, S = mq.shape[:2]
    mk, mv = jnp.split(rmsnorm(mem, mem_norm_w) @ w_mem_kv, 2, axis=-1)
    mq = mq.reshape(B, S, MEM_HEADS, MEM_DH)
    mk = mk.reshape(B, N_MEM, MEM_HEADS, MEM_DH)
    mv = mv.reshape(B, N_MEM, MEM_HEADS, MEM_DH)
    s = jnp.einsum('bshd,bmhd->bhsm', mq, mk).astype(jnp.float32) * (MEM_DH ** -0.5)
    p = jax.nn.softmax(s, axis=-1).astype(mv.dtype)
    return jnp.einsum('bhsm,bmhd->bshd', p, mv).reshape(B, S, MEM_W)


def token_mixing(u, mem, w_in, b_gate, lb, hg_norm_w, mem_norm_w, w_mem_kv, w_br_hg, w_br_dil, w_br_mem, w_out):
    B, S, D = u.shape
    proj = u @ w_in
    hq, hf, hi, hog, dcols, mq, gl = jnp.split(proj, SPLITS, axis=-1)
    o_hg = rmsnorm(hgrn2(hq, hf, hi, lb), hg_norm_w)
    o_hg = o_hg * jax.nn.sigmoid(hog.astype(jnp.float32)).reshape(B, S, HG_HEADS, HG_DV)
    y_hg = o_hg.reshape(B, S, HG_V).astype(u.dtype)
    y_dil = dilated_attention(dcols)
    y_mem = memory_attention(mq, mem, mem_norm_w, w_mem_kv)
    gates = jax.nn.sigmoid((gl + b_gate).astype(jnp.float32)).astype(u.dtype).reshape(B, S, N_BRANCH, D)
    y = (gates[:, :, 0] * (y_hg @ w_br_hg)
         + gates[:, :, 1] * (y_dil @ w_br_dil)
         + gates[:, :, 2] * (y_mem @ w_br_mem))
    return y @ w_out


def setup_inputs(seed: int = 0) -> dict:
    key = jax.random.key(seed)
    ks = jax.random.split(key, 22)
    f32 = jnp.float32

    def w(k, shape, fan_in):
        return jax.random.normal(k, shape, f32) * (fan_in ** -0.5)

    def gain(k, n):
        return 1.0 + 0.05 * jax.random.normal(k, (DEPTH, n), f32)

    return {
        'x': jax.random.normal(ks[0], (BATCH, SEQ, D_MODEL), f32),
        'mem': jax.random.normal(ks[1], (BATCH, N_MEM, D_MODEL), f32),
        'ffn1_pre_w': gain(ks[2], D_MODEL),
        'ffn1_w_gu': w(ks[3], (DEPTH, D_MODEL, 2 * D_FF), D_MODEL),
        'ffn1_w_down': w(ks[4], (DEPTH, D_FF, D_MODEL), D_FF),
        'ffn1_post_w': gain(ks[5], D_MODEL),
        'mix_pre_w': gain(ks[6], D_MODEL),
        'w_in': w(ks[7], (DEPTH, D_MODEL, D_IN), D_MODEL),
        'b_gate': 0.01 * jax.random.normal(ks[8], (DEPTH, GATE_COLS), f32),
        'hg_lb_logits': 0.1 * jax.random.normal(ks[9], (DEPTH + 1, HG_K), f32),
        'hg_norm_w': gain(ks[10], HG_DV),
        'mem_norm_w': gain(ks[11], D_MODEL),
        'w_mem_kv': w(ks[12], (DEPTH, D_MODEL, 2 * MEM_W), D_MODEL),
        'w_br_hg': w(ks[13], (DEPTH, HG_V, D_MODEL), HG_V),
        'w_br_dil': w(ks[14], (DEPTH, DIL_W, D_MODEL), DIL_W),
        'w_br_mem': w(ks[15], (DEPTH, MEM_W, D_MODEL), MEM_W),
        'w_out': w(ks[16], (DEPTH, D_MODEL, D_MODEL), D_MODEL),
        'mix_post_w': gain(ks[17], D_MODEL),
        'ffn2_pre_w': gain(ks[18], D_MODEL),
        'ffn2_w_gu': w(ks[19], (DEPTH, D_MODEL, 2 * D_FF), D_MODEL),
        'ffn2_w_down': w(ks[20], (DEPTH, D_FF, D_MODEL), D_FF),
        'ffn2_post_w': gain(ks[21], D_MODEL),
    }


def reference(x, mem, ffn1_pre_w, ffn1_w_gu, ffn1_w_down, ffn1_post_w, mix_pre_w, w_in, b_gate,
              hg_lb_logits, hg_norm_w, mem_norm_w, w_mem_kv, w_br_hg, w_br_dil, w_br_mem, w_out,
              mix_post_w, ffn2_pre_w, ffn2_w_gu, ffn2_w_down, ffn2_post_w):
    lb_all = jnp.cumsum(jax.nn.softmax(hg_lb_logits.astype(jnp.float32), axis=0), axis=0)
    h = x
    for l in range(DEPTH):
        h = h + 0.5 * rmsnorm(swiglu(rmsnorm(h, ffn1_pre_w[l]), ffn1_w_gu[l], ffn1_w_down[l]), ffn1_post_w[l])
        u = rmsnorm(h, mix_pre_w[l])
        y = token_mixing(u, mem, w_in[l], b_gate[l], lb_all[l], hg_norm_w[l], mem_norm_w[l], w_mem_kv[l],
                         w_br_hg[l], w_br_dil[l], w_br_mem[l], w_out[l])
        h = h + rmsnorm(y, mix_post_w[l])
        h = h + 0.5 * rmsnorm(swiglu(rmsnorm(h, ffn2_pre_w[l]), ffn2_w_gu[l], ffn2_w_down[l]), ffn2_post_w[l])
    return h
```

```python
import numpy as np
from contextlib import ExitStack
import concourse.bass as bass
import concourse.mybir as mybir
from concourse.bass_utils import run_bass_kernel_spmd

F32 = mybir.dt.float32
BF16 = mybir.dt.bfloat16
AF = mybir.ActivationFunctionType
ALU = mybir.AluOpType

D = 1024
T = 512
NFC = 22
EPS = 1e-6
NBLK_TILE = 59
NBLK = 61
SLOTS = 3
NSLOT_AT = 9


class Buf:
    __slots__ = ("name", "w", "r")

    def __init__(self, name):
        self.name = name
        self.w = None
        self.r = {}


class Eng:
    def __init__(self, name, e, sem):
        self.name = name
        self.e = e
        self.sem = sem
        self.count = 0
        self.waited = {}


class DSem:
    def __init__(self, name, sem):
        self.name = name
        self.sem = sem
        self.count = 0


class Ctx:
    def __init__(self, nc, es):
        self.nc = nc
        self.es = es
        mk = lambda n: es.enter_context(nc.semaphore(n))
        self.PE = Eng("pe", nc.tensor, mk("s_pe"))
        self.ACT = Eng("act", nc.scalar, mk("s_act"))
        self.DVE = Eng("dve", nc.vector, mk("s_dve"))
        self.POOL = Eng("pool", nc.gpsimd, mk("s_pool"))
        self.SP = Eng("sp", nc.sync, mk("s_sp"))
        self.sems = {}
        for e in (self.PE, self.ACT, self.DVE, self.POOL, self.SP):
            self.sems[e.name] = e.sem
        self.n_ins = 0
        self.n_pe = 0
        self.marks = []

    def dsem(self, name):
        d = DSem(name, self.es.enter_context(self.nc.semaphore(name)))
        self.sems["d:" + name] = d.sem
        return d

    def _sync(self, eng, R, W):
        need = {}
        for b in R:
            if b.w is not None:
                k, v, src = b.w
                if not (src is eng and eng is self.PE):
                    need[k] = max(need.get(k, 0), v)
        for b in W:
            if b.w is not None:
                k, v, src = b.w
                if not (src is eng and eng is self.PE):
                    need[k] = max(need.get(k, 0), v)
            for (k, v, src) in b.r.values():
                if not (src is eng and eng is self.PE):
                    need[k] = max(need.get(k, 0), v)
        for k, v in need.items():
            if eng.waited.get(k, 0) >= v:
                continue
            if not k.startswith("d:"):
                src = getattr(self, k.upper())
                assert v <= src.count, f"dependency on pending instruction of {k} ({v} > {src.count})"
            eng.e.wait_ge(self.sems[k], v)
            eng.waited[k] = v

    def mark(self, name):
        self.marks.append((name, self.n_pe))

    def op(self, eng, fn, R, W, inc=True):
        self._sync(eng, R, W)
        ins = fn()
        self.n_ins += 1
        if eng is self.PE:
            self.n_pe += 1
        if inc:
            eng.count += 1
            ins.then_inc(eng.sem, 1)
            val = eng.count
        else:
            val = eng.count + 1
        tag = (eng.name, val, eng)
        for b in R:
            old = b.r.get(eng.name)
            if old is None or old[1] < val:
                b.r[eng.name] = tag
        for b in W:
            b.w = tag
            b.r = {}
        return ins

    def dma(self, q, out, in_, R, W, ds):
        self._sync(q, R, W)
        k = "d:" + ds.name
        if ds.count > 0 and q.waited.get(k, 0) < ds.count:
            q.e.wait_ge(ds.sem, ds.count)
            q.waited[k] = ds.count
        ins = q.e.dma_start(out=out, in_=in_)
        ds.count += 16
        ins.then_inc(ds.sem, 16)
        self.n_ins += 1
        tag = (k, ds.count, None)
        for b in R:
            b.r[k] = tag
        for b in W:
            b.w = tag
            b.r = {}

    def barrier(self, dsems=()):
        engs = (self.PE, self.ACT, self.DVE)
        for e in engs + (self.POOL,):
            for e2 in engs:
                if e2 is e or e2.count == 0:
                    continue
                if e.waited.get(e2.name, 0) < e2.count:
                    e.e.wait_ge(e2.sem, e2.count)
                    e.waited[e2.name] = e2.count
            if e is self.POOL:
                continue
            for d in dsems:
                k = "d:" + d.name
                if d.count and e.waited.get(k, 0) < d.count:
                    e.e.wait_ge(d.sem, d.count)
                    e.waited[k] = d.count

    def mm(self, out, lhsT, rhs, R, W, start=True, stop=True, inc=None, **kw):
        if inc is None:
            inc = stop
        return self.op(self.PE, lambda: self.nc.tensor.matmul(out, lhsT, rhs, start=start, stop=stop, **kw), R, W, inc)

    def tr(self, out, in_, ident, R, W, inc=True):
        return self.op(self.PE, lambda: self.nc.tensor.transpose(out, in_, ident), R, W, inc)

    def act(self, out, in_, func, R, W, **kw):
        return self.op(self.ACT, lambda: self.nc.scalar.activation(out=out, in_=in_, func=func, **kw), R, W)

    def tt(self, out, in0, in1, op, R, W):
        return self.op(self.DVE, lambda: self.nc.vector.tensor_tensor(out=out, in0=in0, in1=in1, op=op), R, W)

    def ts(self, out, in0, s1, s2, op0, op1, R, W):
        if op1 is None:
            return self.op(self.DVE, lambda: self.nc.vector.tensor_scalar(out=out, in0=in0, scalar1=s1, scalar2=None, op0=op0), R, W)
        return self.op(self.DVE, lambda: self.nc.vector.tensor_scalar(out=out, in0=in0, scalar1=s1, scalar2=s2, op0=op0, op1=op1), R, W)

    def stt(self, out, in0, scalar, in1, op0, op1, R, W):
        return self.op(self.DVE, lambda: self.nc.vector.scalar_tensor_tensor(out=out, in0=in0, scalar=scalar, in1=in1, op0=op0, op1=op1), R, W)

    def ptt(self, out, in0, in1, op, R, W):
        return self.op(self.POOL, lambda: self.nc.gpsimd.tensor_tensor(out=out, in0=in0, in1=in1, op=op), R, W)

    def recip(self, out, in_, R, W):
        return self.op(self.DVE, lambda: self.nc.vector.reciprocal(out=out, in_=in_), R, W)

    def vcopy(self, out, in_, R, W):
        return self.op(self.DVE, lambda: self.nc.vector.tensor_copy(out=out, in_=in_), R, W)


def build_program(nseq=2, nt=8, dbg=None):
    S = nt * T
    nc = bass.Bass("TRN2", target_bir_lowering=False)
    dt_in = lambda name, shape: nc.dram_tensor(name, shape, F32, kind="ExternalInput").ap()
    x_d = dt_in("x", [nseq, S, D])
    mem_d = dt_in("mem", [nseq, 256, D])
    wall_d = dt_in("wall", [NBLK, 128, 4096])
    gcols_d = dt_in("gcols", [128, 96])
    gpost_d = dt_in("gpost", [3, 128, D])
    cmask_d = dt_in("cmask", [128, 4 * NSLOT_AT * 128 + 512 + 512 + 256])
    out_d = nc.dram_tensor("out", [nseq, S, D], F32, kind="ExternalOutput").ap()
    wb_d = nc.dram_tensor("wb", [NBLK, 128, 4096], BF16, kind="Internal").ap()
    dbg_out = {}

    es = ExitStack()
    with es:
        c = Ctx(nc, es)
        PE, ACT, DVE, POOL, SP = c.PE, c.ACT, c.DVE, c.POOL, c.SP

        uniq = {"n": 0}

        def sb(name, shape, dt, stack=es):
            uniq["n"] += 1
            return stack.enter_context(nc.sbuf_tensor(f"sb{uniq['n']}_{name}", shape, dt))

        ident = sb("ident", [128, 128], BF16)
        ones = sb("ones", [128, 128], BF16)
        zeros = sb("zeros", [128, 128], BF16)
        amask = sb("amask", [128, 4, NSLOT_AT, 128], BF16)
        hmask = sb("hmask", [128, 512], BF16)
        rmask = sb("rmask", [128, 512], F32)
        gcols = sb("gcols", [128, 96], F32)
        lbc = sb("lbc", [128, 16], F32)
        kT0 = sb("kT0", [128, 4, 5, 128], BF16)
        v0 = sb("v0", [128, 5, 512], BF16)
        kT1 = sb("kT1", [128, 4, 2, 512], BF16)
        v1 = sb("v1", [128, 2, 4, 512], BF16)
        kT2 = sb("kT2", [128, 4, 5, 512], BF16)
        v2 = sb("v2", [128, 5, 4, 512], BF16)
        Sst = sb("Sst", [128, 4, 128], F32)
        mkT = sb("mkT", [128, 4, 256], BF16)
        mvT = sb("mvT", [128, 2, 512], BF16)
        h = sb("h", [128, 4, D], F32)
        xT = sb("xT", [128, 8, T], BF16)
        wring = sb("wring", [128, SLOTS, 4096], BF16)
        ybr_box = {}
        stat = sb("stat", [128, 16], F32)
        PS = es.enter_context(nc.psum_tensor("PS", [128, 8, 512], F32))

        B = Buf
        b_const = B("const")
        b_h = [B(f"h{i}") for i in range(4)]
        b_xT = B("xT")
        b_ps = [B(f"ps{i}") for i in range(8)]
        b_slot = [B(f"slot{i}") for i in range(SLOTS)]
        b_kv = {n: B(n) for n in ("k0", "v0", "k1", "v1", "k2", "v2", "S", "mk", "mv", "lb")}
        b_ybr = [B(f"ybr{i}") for i in range(3)]
        b_stat = [B(f"stat{i}") for i in range(16)]

        d_cv = [c.dsem(f"cv{i}") for i in range(8)]
        d_const = c.dsem("cst")
        d_slot = [c.dsem(f"w{i}") for i in range(SLOTS)]
        d_x = [c.dsem(f"x{i}") for i in range(4)]
        d_o = [c.dsem(f"o{i}") for i in range(4)]
        d_misc = c.dsem("misc")
        d_dbg = c.dsem("dbg")
        BDS = (d_dbg, d_misc) + tuple(d_o)

        NM = 4 * NSLOT_AT * 128
        c.dma(POOL, amask[:].rearrange("p a b c -> p (a b c)"), cmask_d[:, 0:NM], [], [b_const], d_const)
        c.dma(POOL, hmask[:], cmask_d[:, NM:NM + 512], [], [b_const], d_const)
        c.dma(POOL, rmask[:], cmask_d[:, NM + 512:NM + 1024], [], [b_const], d_const)
        c.dma(POOL, ident[:], cmask_d[:, NM + 1024:NM + 1152], [], [b_const], d_const)
        c.dma(POOL, ones[:], cmask_d[:, NM + 1152:NM + 1280], [], [b_const], d_const)
        c.dma(POOL, gcols[:], gcols_d[:, :], [], [b_const], d_const)
        b_wbk = [B(f"wb{i}") for i in range(NBLK)]
        cstate = {"ptr": 0, "order": None}

        def conv_upto(n):
            order = cstate["order"]
            while cstate["ptr"] < min(n, len(order)):
                j = cstate["ptr"]
                blk = order[j]
                c.dma(POOL, wb_d[blk], wall_d[blk], [], [b_wbk[blk]], d_cv[j % 8])
                cstate["ptr"] += 1
        c.op(DVE, lambda: nc.vector.memset(zeros[:], 0.0), [], [b_const])
        c.tt(lbc[:, 8:12], gcols[:, 40:44], gcols[:, 44:48], ALU.subtract, [b_const], [b_kv["lb"]])
        c.act(lbc[:, 0:4], lbc[:, 8:12], AF.Sigmoid, [b_kv["lb"]], [b_kv["lb"]])
        c.act(lbc[:, 4:8], lbc[:, 8:12], AF.Sigmoid, [b_kv["lb"]], [b_kv["lb"]], scale=-1.0)
        c.act(lbc[:, 12:16], lbc[:, 4:8], AF.Copy, [b_kv["lb"]], [b_kv["lb"]], scale=-1.0)

        G_FFN1, G_MIX, G_FFN2, G_MEM = 0, 8, 16, 24

        wstate = {"issued": 0, "cur": 0, "seq": []}

        def w_issue():
            i = wstate["issued"]
            if i >= len(wstate["seq"]):
                return
            blk = wstate["seq"][i]
            s = i % SLOTS
            conv_upto(first_use[blk] + 9)
            c.dma(SP, wring[:, s, :], wb_d[blk], [b_wbk[blk]], [b_slot[s]], d_slot[s])
            wstate["issued"] += 1

        def w_get(expect, ahead=0):
            i = wstate["cur"] + ahead
            assert wstate["seq"][i] == expect, (wstate["seq"][i], expect)
            while wstate["issued"] <= i:
                w_issue()
            s = i % SLOTS
            return wring[:, s, :], b_slot[s]

        def w_done():
            wstate["cur"] += 1
            while wstate["issued"] < min(wstate["cur"] + SLOTS, len(wstate["seq"])):
                w_issue()

        tile_order = list(range(17)) + [17, 18, 19, 20] + list(range(21, 30)) + [30]
        merge_order = []
        for half in range(2):
            for b in range(3):
                merge_order += [31 + 2 * b + half, 37 + b]
        tile_order += merge_order + [40, 41] + list(range(42, 59))
        STAGE = build_program.stage
        if STAGE == 1:
            tile_order = list(range(17))
        elif STAGE == 2:
            tile_order = tile_order[:-17]
        for s_ in range(nseq):
            if STAGE >= 2:
                wstate["seq"] += [59, 60]
            for t_ in range(nt):
                wstate["seq"] += tile_order

        cstate["order"] = list(dict.fromkeys(wstate["seq"]))
        assert sorted(cstate["order"]) == list(range(NBLK))
        first_use = {blk: j for j, blk in enumerate(cstate["order"])}

        def dump(name, ap, shape, R, dt=F32):
            if dbg is None or name not in dbg:
                return
            d = nc.dram_tensor("dbg_" + name, shape, dt, kind="ExternalOutput").ap()
            c.dma(SP, d, ap, R, [], d_dbg)
            dbg_out[name] = True

        psrot = {"i": 0}

        def ps_next(pool=(0, 1, 2, 3)):
            i = pool[psrot["i"] % len(pool)]
            psrot["i"] += 1
            return i

        def prenorm(srcs, gcol, dst, dstB, dst_off, stack):
            junk = sb("pn_junk", [128, D], BF16, stack)
            xn = sb("pn_xn", [128, 2, D], BF16, stack)
            b_junk, b_xn = B("junk"), [B("xn0"), B("xn1")]
            for i, (src, sB) in enumerate(srcs):
                st = b_stat[i % 4]
                col = (i % 4) * 3
                c.act(junk[:], src, AF.Square, [sB], [b_junk, st], accum_out=stat[:, col:col + 1])
                c.act(stat[:, col + 1:col + 2], stat[:, col:col + 1], AF.Sqrt, [st, b_const], [st], scale=1.0 / D, bias=epsb[:, 0:1])
                c.recip(stat[:, col + 2:col + 3], stat[:, col + 1:col + 2], [st], [st])
                c.act(xn[:, i % 2, :], src, AF.Copy, [sB, st], [b_xn[i % 2]], scale=stat[:, col + 2:col + 3])
                pi = ps_next()
                pst = PS[:, pi, :].bitcast(BF16)
                for dc in range(8):
                    c.tr(pst[:, dc * 128:(dc + 1) * 128], xn[:, i % 2, dc * 128:(dc + 1) * 128], ident[:],
                         [b_xn[i % 2], b_const], [b_ps[pi]], inc=(dc == 7))
                c.tt(dst[:, :, dst_off + i * 128:dst_off + (i + 1) * 128],
                     pst.rearrange("p (c t) -> p c t", c=8),
                     gcols[:, gcol:gcol + 8].unsqueeze(2).to_broadcast([128, 8, 128]),
                     ALU.mult, [b_ps[pi], b_const], [dstB])

        def postnorm_residual(tc, ysb, b_y, gp, b_gp, coef):
            st = b_stat[4 + tc]
            col = 12 + 0
            col = tc * 3
            st2 = stat2
            c.act(ph["junk2"][:], ysb, AF.Square, [b_y], [b_junk2, st], accum_out=st2[:, col:col + 1])
            k = 1.0 / (coef * coef)
            ec = 1 if coef == 0.5 else 0
            c.act(st2[:, col + 1:col + 2], st2[:, col:col + 1], AF.Sqrt, [st, b_const], [st], scale=k / D, bias=epsb[:, ec:ec + 1])
            c.recip(st2[:, col + 2:col + 3], st2[:, col + 1:col + 2], [st], [st])
            c.tt(ysb, ysb, gp, ALU.mult, [b_y, b_gp], [b_y])
            c.stt(h[:, tc, :], ysb, st2[:, col + 2:col + 3], h[:, tc, :], ALU.mult, ALU.add, [b_y, st, b_h[tc]], [b_h[tc]])

        epsb = sb("epsb", [128, 2], F32)
        c.op(DVE, lambda: nc.vector.memset(epsb[:, 0:1], EPS), [], [b_const])
        c.op(DVE, lambda: nc.vector.memset(epsb[:, 1:2], 4.0 * EPS), [], [b_const])
        stat2 = sb("stat2", [128, 12], F32)
        b_junk2 = B("junk2")
        ph = {}

        def load_gpost(idx, stack):
            gp = sb("gpost", [128, D], F32, stack)
            bg = B("gpost")
            c.dma(POOL, gp[:], gpost_d[idx], [], [bg], d_misc)
            return gp, bg

        def ffn_alloc(stack):
            fb = {"actT": sb("actT", [128, NFC, T], BF16, stack), "sg": sb("ffn_sg", [128, 2, T], F32, stack),
                  "b_act": [B(f"act{j}") for j in range(NFC)], "b_sg": [B("sg0"), B("sg1")]}
            alloc_ystage(stack)
            fb["gp"] = sb("gpost", [128, D], F32, stack)
            fb["b_gp"] = B("gpost")
            return fb

        def ffn(base, gidx, nxt, stack, fb=None, mid_hook=None):
            if fb is None:
                fb = ffn_alloc(stack)
            actT, sg, b_act, b_sg = fb["actT"], fb["sg"], fb["b_act"], fb["b_sg"]
            gp, b_gp = fb["gp"], fb["b_gp"]
            c.dma(POOL, gp[:], gpost_d[gidx], [], [b_gp], d_misc)
            for blk in range(11):
                w, wB = w_get(base + blk)
                w3 = w.rearrange("p (c n) -> p c n", c=8)
                for pi_ in range(2):
                    j = 2 * blk + pi_
                    pg, pu = ps_next(), ps_next()
                    for dc in range(8):
                        c.mm(PS[:, pg, :], w3[:, dc, pi_ * 256:pi_ * 256 + 128], xT[:, dc, :], [wB, b_xT], [b_ps[pg]],
                             start=(dc == 0), stop=(dc == 7))
                    for dc in range(8):
                        c.mm(PS[:, pu, :], w3[:, dc, pi_ * 256 + 128:pi_ * 256 + 256], xT[:, dc, :], [wB, b_xT], [b_ps[pu]],
                             start=(dc == 0), stop=(dc == 7))
                    c.act(sg[:, j % 2, :], PS[:, pg, :], AF.Silu, [b_ps[pg]], [b_sg[j % 2]])
                    c.tt(actT[:, j, :], sg[:, j % 2, :], PS[:, pu, :], ALU.mult, [b_sg[j % 2], b_ps[pu]], [b_act[j]])
                w_done()
            preload_sqrt_table()
            for ch in range(2):
                banks = (4, 5, 6, 7) if ch == 0 else (0, 1, 2, 3)
                for blk in range(3):
                    w, wB = w_get(base + 11 + ch * 3 + blk)
                    w3 = w.rearrange("p (c n) -> p c n", c=8)
                    nf = 8 if blk < 2 else 6
                    for fi in range(nf):
                        f = blk * 8 + fi
                        for tc in range(4):
                            c.mm(PS[:, banks[tc], :], actT[:, f, tc * 128:(tc + 1) * 128], w3[:, fi, :], [b_act[f], wB], [b_ps[banks[tc]]],
                                 start=(f == 0), stop=(f == NFC - 1), inc=(f == NFC - 1 or (fi == nf - 1 and tc == 3)))
                    w_done()
                    if mid_hook is not None and ch == 0 and blk == 1:
                        mid_hook()
                for tc in range(4):
                    evac_y(tc, ch, banks[tc], gp, b_gp)
            c.mark("tail")
            sublayer_tail(0.5, nxt)

        stats = sb("stats", [128, 32], F32)
        b_ssq = [B(f"ssq{i}") for i in range(4)]
        b_ss2 = [B(f"ss2{i}") for i in range(4)]
        b_sA, b_sB = B("sA"), B("sB")
        b_jq = [B("jq0"), B("jq1")]
        b_xn4 = [B(f"xn4{i}") for i in range(4)]
        njq = [0]
        dummy = sb("dummy", [128, 2], F32)
        b_dummy = B("dummy")

        def preload_sqrt_table():
            c.act(dummy[:, 0:1], epsb[:, 0:1], AF.Sqrt, [b_const], [b_dummy])

        def evac_y(tc, ch, bank, gp, b_gp):
            ys = ph["ystage"]
            c.tt(ys[:, tc, ch * 512:(ch + 1) * 512], PS[:, bank, :], gp[:, ch * 512:(ch + 1) * 512], ALU.mult,
                 [b_ps[bank], b_gp], [b_ystage[tc]])
            i2 = njq[0] % 2
            njq[0] += 1
            c.act(ph["junkq"][:, i2, 0:512], PS[:, bank, :], AF.Square, [b_ps[bank], b_ystage[tc]], [b_jq[i2], b_ssq[tc]],
                  accum_out=stats[:, tc * 2 + ch:tc * 2 + ch + 1])

        def sublayer_tail(coef, nxt):
            ys = ph["ystage"]
            k = 1.0 / (coef * coef)
            ec = 1 if coef == 0.5 else 0
            sv = stats[:, 0:8].rearrange("p (t c) -> p t c", c=2)
            c.tt(stats[:, 8:12], sv[:, :, 0], sv[:, :, 1], ALU.add, b_ssq, [b_sA])
            c.act(stats[:, 12:16], stats[:, 8:12], AF.Sqrt, [b_sA, b_const], [b_sA], scale=k / D, bias=epsb[:, ec:ec + 1])
            c.recip(stats[:, 16:20], stats[:, 12:16], [b_sA], [b_sA])
            final = nxt.get("final", False)
            resid, b_resid = nxt.get("resid", (h, b_h))
            for tc in range(4):
                if not final:
                    c.stt(h[:, tc, :], ys[:, tc, :], stats[:, 16 + tc:17 + tc], resid[:, tc, :], ALU.mult, ALU.add,
                          [b_ystage[tc], b_sA, b_resid[tc]] + ([b_h[tc]] if b_resid is not b_h else []), [b_h[tc]])
                else:
                    seq, ti = nxt["seq"], nxt["ti"]
                    t0 = ti * T
                    c.stt(ys[:, tc, :], ys[:, tc, :], stats[:, 16 + tc:17 + tc], h[:, tc, :], ALU.mult, ALU.add,
                          [b_ystage[tc], b_sA, b_h[tc]], [b_ystage[tc]])
                    c.dma(POOL, out_d[seq, t0 + tc * 128:t0 + (tc + 1) * 128, :], ys[:, tc, :], [b_ystage[tc]], [], d_o[tc])
                    if nxt.get("gcol") is not None:
                        t1 = t0 + T
                        c.dma(POOL, h[:, tc, :], x_d[seq, t1 + tc * 128:t1 + (tc + 1) * 128, :], [], [b_h[tc]], d_x[tc])
            gcol = nxt.get("gcol")
            if gcol is None:
                return
            tail_pre(gcol, h, b_h)

        def tail_pre(gcol, h, b_h):
            xn4 = ph["xn4"]
            for tc in range(4):
                c.act(xn4[:, tc, :], h[:, tc, :], AF.Square, [b_h[tc]], [b_xn4[tc], b_ss2[tc]],
                      accum_out=stats[:, 20 + tc:21 + tc])
            c.act(stats[:, 24:28], stats[:, 20:24], AF.Sqrt, b_ss2 + [b_const], [b_sB], scale=1.0 / D, bias=epsb[:, 0:1])
            c.recip(stats[:, 28:32], stats[:, 24:28], [b_sB], [b_sB])
            for tc in range(4):
                if tc < 2:
                    c.act(xn4[:, tc, :], h[:, tc, :], AF.Copy, [b_h[tc], b_sB], [b_xn4[tc]], scale=stats[:, 28 + tc:29 + tc])
                elif True:
                    c.ts(xn4[:, tc, :], h[:, tc, :], stats[:, 28 + tc:29 + tc], None, ALU.mult, None, [b_h[tc], b_sB], [b_xn4[tc]])
                else:
                    c.op(POOL, lambda tc=tc: nc.gpsimd.tensor_scalar(out=xn4[:, tc, :], in0=h[:, tc, :], scalar1=stats[:, 28 + tc:29 + tc],
                                                                      scalar2=None, op0=ALU.mult), [b_h[tc], b_sB], [b_xn4[tc]])
            for tc in range(4):
                pi = ps_next()
                pst = PS[:, pi, :].bitcast(BF16)
                for dc in range(8):
                    c.tr(pst[:, dc * 128:(dc + 1) * 128], xn4[:, tc, dc * 128:(dc + 1) * 128], ident[:],
                         [b_xn4[tc], b_const], [b_ps[pi]], inc=(dc == 7))
                c.tt(xT[:, :, tc * 128:(tc + 1) * 128], pst.rearrange("p (c t) -> p c t", c=8),
                     gcols[:, gcol:gcol + 8].unsqueeze(2).to_broadcast([128, 8, 128]), ALU.mult, [b_ps[pi], b_const], [b_xT])

        b_ystage = [B(f"ys{i}") for i in range(4)]

        def alloc_ystage(stack):
            ph["ystage"] = sb("ystage", [128, 4, D], F32, stack)
            ph["junkq"] = sb("junkq", [128, 2, 512], BF16, stack)
            ph["xn4"] = sb("xn4", [128, 4, D], BF16, stack)

        def ysbuf(tc):
            return ph["ystage"][:, tc, :], b_ystage[tc]

        b_xnext = [B(f"xnext{i}") for i in range(4)]

        def tile(seq, ti):
            if ti == 0:
                c.mark(f"t{ti}.ffn1")
                with ExitStack() as st:
                    ffn(0, 0, {"gcol": G_MIX}, st)
                c.barrier(BDS)
            c.mark(f"t{ti}.mixer")
            mixer(seq, ti)
            c.barrier(BDS)
            c.mark(f"t{ti}.ffn2")
            has_next = ti + 1 < nt
            with ExitStack() as st:
                fb = ffn_alloc(st)
                hook = None
                if has_next:
                    xnext = sb("xnext", [128, 4, D], F32, st)
                    t1 = (ti + 1) * T
                    for tc in range(4):
                        c.dma(POOL, xnext[:, tc, :], x_d[seq, t1 + tc * 128:t1 + (tc + 1) * 128, :], [], [b_xnext[tc]], d_x[tc])
                    hook = lambda: tail_pre(G_FFN1, xnext, b_xnext)
                ffn(42, 2, {"final": True, "seq": seq, "ti": ti, "gcol": None}, st, fb=fb, mid_hook=hook)
                if has_next:
                    c.mark(f"t{ti + 1}.ffn1")
                    ffn(0, 0, {"gcol": G_MIX, "resid": (xnext, b_xnext)}, st, fb=fb)
            c.barrier(BDS)
            c.mark(f"t{ti}.end")

        def first_prenorm(seq):
            for tc in range(4):
                c.dma(POOL, h[:, tc, :], x_d[seq, tc * 128:(tc + 1) * 128, :], [], [b_h[tc]], d_x[tc])
            with ExitStack() as st:
                prenorm([(h[:, tc, :], b_h[tc]) for tc in range(4)], G_FFN1, xT, b_xT, 0, st)
            c.barrier(BDS)

        ppool = {"p": (0, 1, 2, 3)}

        def proj_fm(w3, col0, dst_fn, evac):
            p = ps_next(ppool["p"])
            for dc in range(8):
                c.mm(PS[:, p, :], w3[:, dc, col0:col0 + 128], xT[:, dc, :], [dst_fn, b_xT], [b_ps[p]], start=(dc == 0), stop=(dc == 7))
            evac(p)

        def proj_tm(w3, wB, lhs_fn, evac):
            p = ps_next(ppool["p"])
            for dc in range(8):
                c.mm(PS[:, p, :], lhs_fn(dc), w3[:, dc, :], [wB, b_xT], [b_ps[p]], start=(dc == 0), stop=(dc == 7))
            evac(p)

        def seq_prologue(seq):
            c.op(DVE, lambda: nc.vector.memset(Sst[:].rearrange("p a b -> p (a b)"), 0.0), [], [b_kv["S"]])
            with ExitStack() as st:
                mtmp = sb("memtmp", [128, 2, D], F32, st)
                memT = sb("memT", [128, 8, 256], BF16, st)
                b_mt, b_memT = [B("mt0"), B("mt1")], B("memT")
                for mc in range(2):
                    c.dma(POOL, mtmp[:, mc, :], mem_d[seq, mc * 128:(mc + 1) * 128, :], [], [b_mt[mc]], d_x[mc])
                prenorm([(mtmp[:, mc, :], b_mt[mc]) for mc in range(2)], G_MEM, memT, b_memT, 0, st)
                w, wB = w_get(59)
                w3 = w.rearrange("p (c n) -> p c n", c=8)
                for hh in range(4):
                    p = ps_next()
                    for dc in range(8):
                        c.mm(PS[:, p, 0:256], w3[:, dc, hh * 128:(hh + 1) * 128], memT[:, dc, :], [wB, b_memT], [b_ps[p]],
                             start=(dc == 0), stop=(dc == 7))
                    c.act(mkT[:, hh, :], PS[:, p, 0:256], AF.Copy, [b_ps[p]], [b_kv["mk"]])
                w_done()
                w, wB = w_get(60)
                w3 = w.rearrange("p (c n) -> p c n", c=8)
                for mc in range(2):
                    p = ps_next()
                    for dc in range(8):
                        c.mm(PS[:, p, :], memT[:, dc, mc * 128:(mc + 1) * 128], w3[:, dc, :], [wB, b_memT], [b_ps[p]],
                             start=(dc == 0), stop=(dc == 7))
                    c.act(mvT[:, mc, :], PS[:, p, :], AF.Copy, [b_ps[p]], [b_kv["mv"]])
                w_done()
            c.barrier(BDS)

        def hgrn(ti, st, pgen):
            qs = sb("hg_qs", [128, 4, T], BF16, st)
            sgf = sb("hg_sg", [128, 4, T], F32, st)
            vhg = sb("hg_v", [128, 4, T], BF16, st)
            sgo = sb("hg_sgo", [128, 4, T], BF16, st)
            b_qs, b_sgf, b_vhg, b_sgo = B("qs"), B("sgf"), B("vhg"), B("sgo")
            for bi, (dst, dB, fn) in enumerate(((qs, b_qs, AF.Silu), (sgf, b_sgf, AF.Sigmoid), (None, None, None), (sgo, b_sgo, AF.Sigmoid))):
                w, wB = w_get(17 + bi)
                w3 = w.rearrange("p (c n) -> p c n", c=8)
                if dst is not None:
                    for hh in range(4):
                        proj_fm(w3, hh * 128, wB, lambda p, hh=hh: c.act(dst[:, hh, :], PS[:, p, :], fn, [b_ps[p]], [dB]))
                else:
                    for tc in range(4):
                        proj_tm(w3, wB, lambda dc, tc=tc: xT[:, dc, tc * 128:(tc + 1) * 128],
                                lambda p, tc=tc: c.act(vhg[:, tc, :], PS[:, p, :], AF.Copy, [b_ps[p]], [b_vhg]))
                w_done()
            lf = sb("hg_lf", [128, T], F32, st)
            bb = sb("hg_b", [128, T], F32, st)
            bm = sb("hg_bm", [128, T], F32, st)
            E1 = sb("hg_E1", [128, T], F32, st)
            qe = sb("hg_qe", [128, 2, T], BF16, st)
            ke = sb("hg_ke", [128, 2, T], BF16, st)
            keTM = sb("hg_keTM", [128, 2, T], BF16, st)
            scT = sb("hg_scT", [128, 2, T], BF16, st)
            dece = sb("hg_dec", [128, 2, 24], F32, st)
            Sp = sb("hg_Sp", [128, 1, 8, 128], BF16, st)
            tkv = sb("hg_tkv", [128, 2, 8, 128], F32, st)
            bl, bbB, bbm, bE1 = [B(n) for n in ("lf", "bb", "bm", "E1")]
            E2, bE2 = bm, bbm
            kk, bkk = lf, bl
            rt, y1 = E1, E2
            bqe, bke, bkeTM, bscT, bdec, bSp, btkv = [[B(n + "0"), B(n + "1")] for n in ("qe", "ke", "keTM", "scT", "dec", "Sp", "tkv")]
            bosq, brt, by1 = bl, bE1, bE2
            osq_v = lf[:].bitcast(BF16)[:, 0:T]
            b3 = bb[:].rearrange("p (c t) -> p c t", t=64)
            bm3 = bm[:].rearrange("p (c t) -> p c t", t=64)
            PKV = (2, 3)

            def stageA(hh):
                i2 = hh % 2
                oml = lbc[:, 4 + hh:5 + hh]
                lbh = lbc[:, hh:hh + 1]
                c.act(lf[:], sgf[:, hh, :], AF.Ln, [b_sgf, b_kv["lb"]], [bl], scale=oml, bias=lbh)
                c.op(DVE, lambda: nc.vector.tensor_tensor_scan(out=bb[:], data0=rmask[:], data1=lf[:], initial=0.0,
                                                               op0=ALU.mult, op1=ALU.add), [bl, b_const], [bbB])
                c.tt(bm3, b3, b3[:, :, 31:32].to_broadcast([128, 8, 64]), ALU.subtract, [bbB], [bbm])
                c.act(E1[:], bm[:], AF.Exp, [bbm], [bE1])
                c.act(dece[:, i2, 16:24], bm3[:, :, 63], AF.Exp, [bbm], [bdec[i2]])
                c.act(E2[:], bm[:], AF.Exp, [bbm], [bE2], scale=-1.0)
                c.act(dece[:, i2, 0:8], b3[:, :, 63], AF.Exp, [bbB], [bdec[i2]])
                c.act(dece[:, i2, 8:16], b3[:, :, 31], AF.Exp, [bbB], [bdec[i2]])
                c.tt(qe[:, i2, :], qs[:, hh, :], E1[:], ALU.mult, [b_qs, bE1], [bqe[i2]])
                c.act(kk[:], sgf[:, hh, :], AF.Identity, [b_sgf, b_kv["lb"]], [bkk], scale=lbc[:, 12 + hh:13 + hh], bias=oml)
                c.tt(ke[:, i2, :], kk[:], E2[:], ALU.mult, [bkk, bE2], [bke[i2]])

            def P(n):
                for _ in range(n):
                    next(pgen, None)

            def stageAp(hh):
                i2 = hh % 2
                P(2)
                p = ps_next((0, 1))
                pst = PS[:, p, :].bitcast(BF16)
                for j in range(4):
                    c.tr(pst[:, j * 128:(j + 1) * 128], ke[:, i2, j * 128:(j + 1) * 128], ident[:], [bke[i2], b_const], [b_ps[p]], inc=(j == 3))
                c.act(keTM[:, i2, :], pst[:, 0:512], AF.Copy, [b_ps[p]], [bkeTM[i2]])
                p = ps_next((0, 1))
                for j in range(4):
                    c.mm(PS[:, p, j * 128:(j + 1) * 128], ke[:, i2, j * 128:(j + 1) * 128], qe[:, i2, j * 128:(j + 1) * 128],
                         [bke[i2], bqe[i2]], [b_ps[p]], start=True, stop=True, inc=(j == 3))
                c.tt(scT[:, i2, :], PS[:, p, :], hmask[:], ALU.mult, [b_ps[p], b_const], [bscT[i2]])
                P(2)
                for cc in range(8):
                    j, par = cc // 2, cc % 2
                    lo = par * 64
                    c.mm(PS[:, PKV[par], j * 128:(j + 1) * 128], keTM[lo:lo + 64, i2, j * 128:(j + 1) * 128],
                         vhg[lo:lo + 64, j, hh * 128:(hh + 1) * 128],
                         [bkeTM[i2], b_vhg], [b_ps[PKV[par]]], start=True, stop=True, inc=(cc >= 6))
                for par in range(2):
                    c.tt(tkv[:, i2, :, :].rearrange("p (j r) v -> p r j v", r=2)[:, par],
                         PS[:, PKV[par], :].rearrange("p (j v) -> p j v", v=128),
                         dece[:, i2, 16:24].rearrange("p (j r) -> p r j", r=2)[:, par].unsqueeze(2).to_broadcast([128, 4, 128]),
                         ALU.mult, [b_ps[PKV[par]], bdec[i2]], [btkv[i2]])

            def stageB(hh):
                i2 = hh % 2
                if build_program.skipB:
                    return
                for cc in range(8):
                    c.ts(Sp[:, 0, cc, :], Sst[:, hh, :], dece[:, i2, 8 + cc:9 + cc], None, ALU.mult, None, [b_kv["S"], bdec[i2]], [bSp[0]])
                    c.stt(Sst[:, hh, :], Sst[:, hh, :], dece[:, i2, cc:cc + 1], tkv[:, i2, cc, :], ALU.mult, ALU.add,
                          [b_kv["S"], bdec[i2], btkv[i2]], [b_kv["S"]])

            def stageBp(hh):
                i2 = hh % 2
                P(3)
                po = 4 + (hh % 2)
                for j in range(4):
                    c.mm(PS[:, po, j * 128:(j + 1) * 128], vhg[:, j, hh * 128:(hh + 1) * 128], scT[:, i2, j * 128:(j + 1) * 128],
                         [b_vhg, bscT[i2]], [b_ps[po]], start=True, stop=False, inc=False, skip_group_check=True)
                    for cc in (2 * j, 2 * j + 1):
                        c.mm(PS[:, po, cc * 64:(cc + 1) * 64], Sp[:, 0, cc, :], qe[:, i2, cc * 64:(cc + 1) * 64], [bSp[0], bqe[i2]], [b_ps[po]],
                             start=False, stop=True, inc=(cc == 7), skip_group_check=True)
                c.act(osq_v, PS[:, po, :], AF.Square, [b_ps[po]], [bosq])
                P(3)
                pn = ps_next((0, 1))
                c.mm(PS[:, pn, :], ones[:], osq_v, [b_const, bosq], [b_ps[pn]], start=True, stop=True)
                c.act(rt[:], PS[:, pn, :], AF.Ln, [b_ps[pn], b_const], [brt], scale=1.0 / 128, bias=epsb[:, 0:1])
                c.act(rt[:], rt[:], AF.Exp, [brt], [brt], scale=-0.5)
                c.stt(y1[:], PS[:, po, :], gcols[:, 32:33], rt[:], ALU.mult, ALU.mult, [b_ps[po], brt, b_const], [by1])
                c.tt(ybr_box[0][:, hh, :], y1[:], sgo[:, hh, :], ALU.mult, [by1, b_sgo], [b_ybr[0]])

            c.mark("hgrn_stages")
            ppool["p"] = (6, 7)
            for fn, a in ((stageA, 0), (stageAp, 0), (stageA, 1), (stageAp, 1), (stageB, 0), (stageBp, 0),
                          (stageA, 2), (stageAp, 2), (stageB, 1), (stageBp, 1), (stageA, 3), (stageAp, 3),
                          (stageB, 2), (stageBp, 2), (stageB, 3), (stageBp, 3)):
                fn(a)
            for _ in pgen:
                pass
            ppool["p"] = (0, 1, 2, 3)

        def attn_proj_gen(ti, actx):
            qg, mq = actx["qg"], actx["mq"]
            b_q = [B(f"q{g}") for g in range(3)]
            b_mq = B("mq")
            gc0 = ti * 4
            kcache = (kT0, kT1, kT2)
            kB = (b_kv["k0"], b_kv["k1"], b_kv["k2"])
            vB = (b_kv["v0"], b_kv["v1"], b_kv["v2"])
            nev = [0]
            actx.update(qg=qg, mq=mq, b_q=b_q, b_mq=b_mq, kB=kB, vB=vB, gc0=gc0)

            def evac(out, in_, R, W):
                nev[0] += 1
                if nev[0] % 2:
                    c.act(out, in_, AF.Copy, R, W)
                else:
                    c.vcopy(out, in_, R, W)

            def perm(ap):
                return ap.rearrange("p (s r) -> p r s", r=4)

            def unperm(ap):
                return ap.rearrange("p (r s) -> p r s", r=4)

            for g in range(3):
                w, wB = w_get(21 + g * 3)
                w3 = w.rearrange("p (c n) -> p c n", c=8)
                for hh in range(4):
                    if g == 0:
                        proj_fm(w3, hh * 128, wB, lambda p, hh=hh: evac(qg[0][:, hh, :], PS[:, p, :], [b_ps[p]], [b_q[0]]))
                        yield
                    else:
                        proj_fm(w3, hh * 128, wB, lambda p, hh=hh, g=g: evac(unperm(qg[g][:, hh, :]), perm(PS[:, p, :]), [b_ps[p]], [b_q[g]]))
                        yield
                w_done()
                w, wB = w_get(22 + g * 3)
                w3 = w.rearrange("p (c n) -> p c n", c=8)
                for hh in range(4):
                    if g == 0:
                        def ev(p, hh=hh):
                            for cq in range(4):
                                evac(kT0[:, hh, (gc0 + cq) % 5, :], PS[:, p, cq * 128:(cq + 1) * 128], [b_ps[p]], [kB[0]])
                        proj_fm(w3, hh * 128, wB, ev)
                        yield
                    else:
                        slot = ti % (2 if g == 1 else 5)
                        proj_fm(w3, hh * 128, wB, lambda p, hh=hh, g=g, slot=slot:
                                evac(unperm(kcache[g][:, hh, slot, :]), perm(PS[:, p, :]), [b_ps[p]], [kB[g]]))
                        yield
                w_done()
                w, wB = w_get(23 + g * 3)
                w3 = w.rearrange("p (c n) -> p c n", c=8)
                for tc in range(4):
                    if g == 0:
                        proj_tm(w3, wB, lambda dc, tc=tc: xT[:, dc, tc * 128:(tc + 1) * 128],
                                lambda p, tc=tc: evac(v0[:, (gc0 + tc) % 5, :], PS[:, p, :], [b_ps[p]], [vB[0]]))
                        yield
                    elif g == 1:
                        proj_tm(w3, wB, lambda dc, tc=tc: xT[:, dc, tc::4],
                                lambda p, tc=tc: evac(v1[:, ti % 2, tc, :], PS[:, p, :], [b_ps[p]], [vB[1]]))
                        yield
                    else:
                        proj_tm(w3, wB, lambda dc, tc=tc: xT[:, dc, tc::4],
                                lambda p, tc=tc: evac(v2[:, ti % 5, tc, :], PS[:, p, :], [b_ps[p]], [vB[2]]))
                        yield
                w_done()
            w, wB = w_get(30)
            w3 = w.rearrange("p (c n) -> p c n", c=8)
            for hh in range(4):
                proj_fm(w3, hh * 128, wB, lambda p, hh=hh: evac(mq[:, hh, :], PS[:, p, :], [b_ps[p]], [b_mq]))
                yield
            w_done()

        def attn(ti, st, actx):
            qg, mq, b_q, b_mq, kB, vB, gc0 = [actx[n] for n in ("qg", "mq", "b_q", "b_mq", "kB", "vB", "gc0")]

            def perm(ap):
                return ap.rearrange("p (s r) -> p r s", r=4)

            ebuf = sb("at_e", [128, 2, NSLOT_AT * 128], BF16, st)
            pbuf = sb("at_p", [128, 2, NSLOT_AT * 128], BF16, st)
            rden = sb("at_rden", [128, T], F32, st)
            b_e, b_p, b_rden = [B("e0"), B("e1")], [B("p0"), B("p1")], B("rden")
            it = [0]
            PN, PD = 6, 7
            numv = PS[:, PN, :]
            denv = PS[:, PD, :]
            scale = 128.0 ** -0.5
            items = []

            def dil_item(hh, qc, idx):
                blocks = []
                for dist, ms in ((4, 0), (3, 1), (2, 2), (1, 3)):
                    if ti - dist >= 0:
                        sl = (ti - dist) % 5
                        blocks.append((ms, kT2[:, hh, sl, qc * 128:(qc + 1) * 128], v2[:, sl, qc, hh * 128:(hh + 1) * 128], 2, False))
                if ti >= 1:
                    sl = (ti - 1) % 2
                    blocks.append((4, kT1[:, hh, sl, qc * 128:(qc + 1) * 128], v1[:, sl, qc, hh * 128:(hh + 1) * 128], 1, False))
                if gc0 + qc >= 1:
                    sl = (gc0 + qc - 1) % 5
                    blocks.append((5, kT0[:, hh, sl, :], v0[:, sl, hh * 128:(hh + 1) * 128], 0, True))
                sl = (gc0 + qc) % 5
                blocks.append((6, kT0[:, hh, sl, :], v0[:, sl, hh * 128:(hh + 1) * 128], 0, True))
                blocks.append((7, kT1[:, hh, ti % 2, qc * 128:(qc + 1) * 128], v1[:, ti % 2, qc, hh * 128:(hh + 1) * 128], 1, False))
                blocks.append((8, kT2[:, hh, ti % 5, qc * 128:(qc + 1) * 128], v2[:, ti % 5, qc, hh * 128:(hh + 1) * 128], 2, False))
                i2 = idx % 2
                sbase = 3 * i2
                scv = PS[:, sbase:sbase + 3, :].rearrange("p a b -> p (a b)")
                scB = [b_ps[sbase], b_ps[sbase + 1], b_ps[sbase + 2]]

                def A():
                    for bi, (ms, kap, vap, g, nat) in enumerate(blocks):
                        c.mm(scv[:, ms * 128:(ms + 1) * 128], kap, qg[g][:, hh, qc * 128:(qc + 1) * 128], [kB[g], b_q[g]], scB,
                             start=True, stop=True, inc=(bi == len(blocks) - 1))
                    lo = blocks[0][0] * 128
                    hi = NSLOT_AT * 128
                    c.act(ebuf[:, i2, lo:hi], scv[:, lo:hi], AF.Exp, scB, [b_e[i2]], scale=scale)
                    c.tt(pbuf[:, i2, lo:hi], ebuf[:, i2, lo:hi], amask[:, hh, :, :].rearrange("p a b -> p (a b)")[:, lo:hi], ALU.mult,
                         [b_e[i2], b_const], [b_p[i2]])

                def Bf():
                    if qc == 0:
                        c.mm(numv, zeros[:], qg[0][:, hh, :], [b_const, b_q[0]], [b_ps[PN]], start=True, stop=False, inc=False, skip_group_check=True)
                        c.mm(denv, zeros[:], qg[0][:, hh, :], [b_const, b_q[0]], [b_ps[PD]], start=True, stop=False, inc=False, skip_group_check=True)
                    for bi, (ms, kap, vap, g, nat) in enumerate(blocks):
                        if nat:
                            no, do = numv[:, qc * 128:(qc + 1) * 128], denv[:, qc * 128:(qc + 1) * 128]
                        else:
                            no, do = perm(numv)[:, qc, :], perm(denv)[:, qc, :]
                        last = (bi == len(blocks) - 1)
                        pv = pbuf[:, i2, ms * 128:(ms + 1) * 128]
                        c.mm(no, vap, pv, [vB[g], b_p[i2]], [b_ps[PN]], start=False, stop=last, inc=last, skip_group_check=True)
                        c.mm(do, ones[:], pv, [b_const, b_p[i2]], [b_ps[PD]], start=False, stop=last, inc=last, skip_group_check=True)
                    if qc == 3:
                        c.act(rden[:], denv, AF.Ln, [b_ps[PD]], [b_rden])
                        c.act(rden[:], rden[:], AF.Exp, [b_rden], [b_rden], scale=-1.0)
                        c.tt(ybr_box[1][:, hh, :], numv, rden[:], ALU.mult, [b_ps[PN], b_rden], [b_ybr[1]])
                return A, Bf

            def mem_item(hh, idx):
                i2 = idx % 2
                sbase = 3 * i2
                scv = PS[:, sbase:sbase + 3, :].rearrange("p a b -> p (a b)")
                scB = [b_ps[sbase], b_ps[sbase + 1], b_ps[sbase + 2]]

                def A():
                    for mc in range(2):
                        c.mm(scv[:, mc * 512:(mc + 1) * 512], mkT[:, hh, mc * 128:(mc + 1) * 128], mq[:, hh, :], [b_kv["mk"], b_mq], scB,
                             start=True, stop=True, inc=(mc == 1))
                    c.act(ebuf[:, i2, 0:1024], scv[:, 0:1024], AF.Exp, scB, [b_e[i2]], scale=scale)

                def Bf():
                    for mc in range(2):
                        c.mm(numv, mvT[:, mc, hh * 128:(hh + 1) * 128], ebuf[:, i2, mc * 512:(mc + 1) * 512], [b_kv["mv"], b_e[i2]], [b_ps[PN]],
                             start=(mc == 0), stop=(mc == 1))
                        c.mm(denv, ones[:], ebuf[:, i2, mc * 512:(mc + 1) * 512], [b_const, b_e[i2]], [b_ps[PD]],
                             start=(mc == 0), stop=(mc == 1))
                    c.act(rden[:], denv, AF.Ln, [b_ps[PD]], [b_rden])
                    c.act(rden[:], rden[:], AF.Exp, [b_rden], [b_rden], scale=-1.0)
                    c.tt(ybr_box[2][:, hh, :], numv, rden[:], ALU.mult, [b_ps[PN], b_rden], [b_ybr[2]])
                return A, Bf

            for hh in range(4):
                for qc in range(4):
                    items.append(dil_item(hh, qc, len(items)))
                items.append(mem_item(hh, len(items)))
            items[0][0]()
            for i in range(1, len(items)):
                items[i][0]()
                items[i - 1][1]()
            items[-1][1]()

        def merge(st):
            yT = sb("mg_yT", [128, 8, T], BF16, st)
            acc = sb("mg_acc", [128, 4, T], F32, st)
            sgt = sb("mg_sg", [128, 2, T], F32, st)
            tmp = sb("mg_tmp", [128, 2, T], F32, st)
            b_yT, b_acc = B("yT"), [B(f"acc{i}") for i in range(4)]
            b_sgt, b_tmp = [B("sgt0"), B("sgt1")], [B("tmp0"), B("tmp1")]
            gp, b_gp = load_gpost(1, st)
            alloc_ystage(st)
            n = 0
            for half in range(2):
                for b in range(3):
                    wg, wgB = w_get(31 + 2 * b + half)
                    wr, wrB = w_get(37 + b, ahead=1)
                    wg3 = wg.rearrange("p (c n) -> p c n", c=8)
                    wr3 = wr.rearrange("p (k n) -> p k n", k=4)
                    for o4 in range(4):
                        oc = half * 4 + o4
                        i2 = n % 2
                        n += 1
                        pg, pz = ps_next(), ps_next()
                        for dc in range(8):
                            c.mm(PS[:, pg, :], wg3[:, dc, o4 * 128:(o4 + 1) * 128], xT[:, dc, :], [wgB, b_xT], [b_ps[pg]],
                                 start=(dc == 0), stop=(dc == 7))
                        for kc in range(4):
                            c.mm(PS[:, pz, :], wr3[:, kc, oc * 128:(oc + 1) * 128], ybr_box[b][:, kc, :], [wrB, b_ybr[b]], [b_ps[pz]],
                                 start=(kc == 0), stop=(kc == 3))
                        bcol = 48 + b * 8 + oc
                        c.act(sgt[:, i2, :], PS[:, pg, :], AF.Sigmoid, [b_ps[pg], b_const], [b_sgt[i2]], bias=gcols[:, bcol:bcol + 1])
                        if b == 0:
                            c.tt(acc[:, o4, :], sgt[:, i2, :], PS[:, pz, :], ALU.mult, [b_sgt[i2], b_ps[pz]], [b_acc[o4]])
                        else:
                            c.tt(tmp[:, i2, :], sgt[:, i2, :], PS[:, pz, :], ALU.mult, [b_sgt[i2], b_ps[pz]], [b_tmp[i2]])
                            if b == 1:
                                c.tt(acc[:, o4, :], acc[:, o4, :], tmp[:, i2, :], ALU.add, [b_acc[o4], b_tmp[i2]], [b_acc[o4]])
                            else:
                                c.tt(yT[:, oc, :], acc[:, o4, :], tmp[:, i2, :], ALU.add, [b_acc[o4], b_tmp[i2]], [b_yT])
                    w_done()
                    w_done()
            preload_sqrt_table()
            for ch in range(2):
                w, wB = w_get(40 + ch)
                w3 = w.rearrange("p (c n) -> p c n", c=8)
                bk = (4, 5, 6, 7) if ch == 0 else (0, 1, 2, 3)
                for tc in range(4):
                    for kc in range(8):
                        c.mm(PS[:, bk[tc], :], yT[:, kc, tc * 128:(tc + 1) * 128], w3[:, kc, :], [b_yT, wB], [b_ps[bk[tc]]],
                             start=(kc == 0), stop=(kc == 7))
                w_done()
                for tc in range(4):
                    evac_y(tc, ch, bk[tc], gp, b_gp)
            c.mark("tail")
            sublayer_tail(1.0, {"gcol": G_FFN2})

        def mixer(seq, ti):
            with ExitStack() as mst:
                ybr_box[0] = sb("ybr_hg", [128, 4, T], BF16, mst)
                ybr_box[1] = sb("ybr_dil", [128, 4, T], BF16, mst)
                ybr_box[2] = sb("ybr_mem", [128, 4, T], BF16, mst)
                with ExitStack() as qst:
                    actx = {}
                    actx["qg"] = [sb(f"at_q{g}", [128, 4, T], BF16, qst) for g in range(3)]
                    actx["mq"] = sb("at_mq", [128, 4, T], BF16, qst)
                    pgen = attn_proj_gen(ti, actx)
                    with ExitStack() as st:
                        hgrn(ti, st, pgen)
                    c.barrier(BDS)
                    c.mark(f"t{ti}.attn")
                    with ExitStack() as st:
                        attn(ti, st, actx)
                    c.barrier(BDS)
                    c.mark(f"t{ti}.merge")
                if ti == build_program.dump_tile:
                    for bi, nm in enumerate(("yhg", "ydil", "ymem")):
                        dump(nm, ybr_box[bi][:, :, :], [128, 4, T], [b_ybr[bi]], BF16)
                with ExitStack() as st:
                    merge(st)

        for seq in range(nseq):
            seq_prologue(seq)
            first_prenorm(seq)
            for ti in range(nt):
                tile(seq, ti)
        for d in d_o:
            SP.e.wait_ge(d.sem, d.count)
        if d_dbg.count:
            SP.e.wait_ge(d_dbg.sem, d_dbg.count)
        build_program.n_ins = c.n_ins
        build_program.marks = c.marks
    return nc, dbg_out


build_program.stage = 3
build_program.skipB = False
build_program.dump_tile = -1


def _blk_cols(w, cols):
    sub = w[:, cols]
    return np.ascontiguousarray(sub.reshape(8, 128, sub.shape[1]).transpose(1, 0, 2)).reshape(128, -1)


def _prep_weights(inp):
    blocks = []

    def ffn_blocks(wgu, wdn):
        out = []
        for b in range(11):
            cols = []
            for j in (2 * b, 2 * b + 1):
                cols += list(range(j * 128, (j + 1) * 128)) + list(range(2816 + j * 128, 2816 + (j + 1) * 128))
            out.append(_blk_cols(wgu, np.array(cols)))
        for ch in range(2):
            for b in range(3):
                blk = np.zeros((128, 8, 512), np.float32)
                nf = 8 if b < 2 else 6
                for fi in range(nf):
                    f = b * 8 + fi
                    blk[:, fi, :] = wdn[f * 128:(f + 1) * 128, ch * 512:(ch + 1) * 512]
                out.append(blk.reshape(128, -1))
        return out

    blocks += ffn_blocks(inp["ffn1_w_gu"][0], inp["ffn1_w_down"][0])
    w_in = inp["w_in"][0]
    for b in range(14):
        blocks.append(_blk_cols(w_in, np.arange(b * 512, (b + 1) * 512)))
    for b in range(6):
        blocks.append(_blk_cols(w_in, np.arange(7168 + b * 512, 7168 + (b + 1) * 512)))
    for name in ("w_br_hg", "w_br_dil", "w_br_mem"):
        w = inp[name][0]
        blocks.append(np.ascontiguousarray(w.reshape(4, 128, 1024).transpose(1, 0, 2)).reshape(128, -1))
    w_out = inp["w_out"][0]
    for ch in range(2):
        blocks.append(_blk_cols(w_out, np.arange(ch * 512, (ch + 1) * 512)))
    blocks += ffn_blocks(inp["ffn2_w_gu"][0], inp["ffn2_w_down"][0])
    wm = inp["w_mem_kv"][0]
    for ch in range(2):
        blocks.append(_blk_cols(wm, np.arange(ch * 512, (ch + 1) * 512)))
    assert len(blocks) == NBLK
    return np.stack(blocks).astype(np.float32)


def _prep_gcols(inp):
    g = np.zeros((128, 96), np.float32)
    fm = lambda v: np.asarray(v, np.float32).reshape(8, 128).T
    g[:, 0:8] = fm(inp["ffn1_pre_w"][0])
    g[:, 8:16] = fm(inp["mix_pre_w"][0])
    g[:, 16:24] = fm(inp["ffn2_pre_w"][0])
    g[:, 24:32] = fm(inp["mem_norm_w"][0])
    g[:, 32] = np.asarray(inp["hg_norm_w"][0], np.float32)
    lbl = np.asarray(inp["hg_lb_logits"], np.float32)
    g[:, 40:44] = lbl[0].reshape(4, 128).T
    g[:, 44:48] = lbl[1].reshape(4, 128).T
    g[:, 48:72] = np.asarray(inp["b_gate"][0], np.float32).reshape(24, 128).T
    return g


def _prep_consts():
    NM = 4 * NSLOT_AT * 128
    cm = np.zeros((128, NM + 512 + 512 + 256), np.float32)
    am = np.zeros((128, 4, NSLOT_AT, 128), np.float32)
    kj = np.arange(128)[:, None].astype(np.float64)
    qi = np.arange(128)[None, :].astype(np.float64)
    for hh in range(4):
        sl = [2.0 ** (-8.0 * (g * 4 + hh + 1) / 12.0) for g in range(3)]
        d = qi - kj
        am[:, hh, 6] = np.where(d >= 0, np.exp(-sl[0] * d), 0.0)
        d = 128 + qi - kj
        am[:, hh, 5] = np.where(d <= 128, np.exp(-sl[0] * d), 0.0)
        d = qi - kj
        am[:, hh, 7] = np.where(d >= 0, np.exp(-sl[1] * 4 * d), 0.0)
        d = 128 + qi - kj
        am[:, hh, 4] = np.where(d <= 128, np.exp(-sl[1] * 4 * d), 0.0)
        same = (np.arange(128)[:, None] % 4) == (np.arange(128)[None, :] % 4)
        aq = np.floor(qi / 4)
        ak = np.floor(kj / 4)
        for dist, slot in ((4, 0), (3, 1), (2, 2), (1, 3), (0, 8)):
            d = 32 * dist + aq - ak
            ok = same & (d >= 0) & (d <= 128)
            am[:, hh, slot] = np.where(ok, np.exp(-sl[2] * 16 * d), 0.0)
    cm[:, 0:NM] = am.reshape(128, -1)
    s = np.arange(128)[:, None]
    cc = np.arange(128)[None, :]
    hm = ((s // 64) == (cc // 64)) & (s <= cc)
    cm[:, NM:NM + 512] = np.tile(hm.astype(np.float32), (1, 4))
    rm = np.ones((128, 512), np.float32)
    rm[:, 0::64] = 0.0
    cm[:, NM + 512:NM + 1024] = rm
    cm[:, NM + 1024:NM + 1152] = np.eye(128, dtype=np.float32)
    cm[:, NM + 1152:NM + 1280] = 1.0
    return cm


def make_in_maps(inp, ncores, nseq, nt):
    S = nt * T
    wall = _prep_weights(inp)
    gcols = _prep_gcols(inp)
    gpost = np.stack([np.broadcast_to(np.asarray(inp[n][0], np.float32)[None, :], (128, D)).copy()
                      for n in ("ffn1_post_w", "mix_post_w", "ffn2_post_w")])
    cmask = _prep_consts()
    x = np.asarray(inp["x"], np.float32)
    mem = np.asarray(inp["mem"], np.float32)
    maps = []
    for i in range(ncores):
        maps.append({
            "x": np.ascontiguousarray(x[i * nseq:(i + 1) * nseq, :S]),
            "mem": np.ascontiguousarray(mem[i * nseq:(i + 1) * nseq]),
            "wall": wall, "gcols": gcols, "gpost": gpost, "cmask": cmask,
        })
    return maps


def kernel(**inputs):
    inp = {k: np.asarray(v) for k, v in inputs.items()}
    ncores, nseq, nt = 8, 2, 8
    nc, _ = build_program(nseq, nt)
    maps = make_in_maps(inp, ncores, nseq, nt)
    res = run_bass_kernel_spmd(nc, maps, core_ids=list(range(ncores)))
    out = np.concatenate([np.asarray(r["out"]) for r in res.results], axis=0)
    return out.astype(np.float32)
```

```python
import numpy as np
from contextlib import ExitStack
import concourse.bass as bass
import concourse.mybir as mybir
from concourse.bass_utils import run_bass_kernel_spmd

F32 = mybir.dt.float32
BF16 = mybir.dt.bfloat16
AF = mybir.ActivationFunctionType
ALU = mybir.AluOpType

D = 1024
T = 512
NFC = 22
EPS = 1e-6
NBLK_TILE = 59
NBLK = 61
SLOTS = 3
NSLOT_AT = 9


class Buf:
    __slots__ = ("name", "w", "r")

    def __init__(self, name):
        self.name = name
        self.w = None
        self.r = {}


class Eng:
    def __init__(self, name, e, sem):
        self.name = name
        self.e = e
        self.sem = sem
        self.count = 0
        self.waited = {}


class DSem:
    def __init__(self, name, sem):
        self.name = name
        self.sem = sem
        self.count = 0


class Ctx:
    def __init__(self, nc, es):
        self.nc = nc
        self.es = es
        mk = lambda n: es.enter_context(nc.semaphore(n))
        self.PE = Eng("pe", nc.tensor, mk("s_pe"))
        self.ACT = Eng("act", nc.scalar, mk("s_act"))
        self.DVE = Eng("dve", nc.vector, mk("s_dve"))
        self.POOL = Eng("pool", nc.gpsimd, mk("s_pool"))
        self.SP = Eng("sp", nc.sync, mk("s_sp"))
        self.sems = {}
        for e in (self.PE, self.ACT, self.DVE, self.POOL, self.SP):
            self.sems[e.name] = e.sem
        self.n_ins = 0
        self.n_pe = 0
        self.marks = []

    def dsem(self, name):
        d = DSem(name, self.es.enter_context(self.nc.semaphore(name)))
        self.sems["d:" + name] = d.sem
        return d

    def _sync(self, eng, R, W):
        need = {}
        for b in R:
            if b.w is not None:
                k, v, src = b.w
                if not (src is eng and eng is self.PE):
                    need[k] = max(need.get(k, 0), v)
        for b in W:
            if b.w is not None:
                k, v, src = b.w
                if not (src is eng and eng is self.PE):
                    need[k] = max(need.get(k, 0), v)
            for (k, v, src) in b.r.values():
                if not (src is eng and eng is self.PE):
                    need[k] = max(need.get(k, 0), v)
        for k, v in need.items():
            if eng.waited.get(k, 0) >= v:
                continue
            if not k.startswith("d:"):
                src = getattr(self, k.upper())
                assert v <= src.count, f"dependency on pending instruction of {k} ({v} > {src.count})"
            eng.e.wait_ge(self.sems[k], v)
            eng.waited[k] = v

    def mark(self, name):
        self.marks.append((name, self.n_pe))

    def op(self, eng, fn, R, W, inc=True):
        self._sync(eng, R, W)
        ins = fn()
        self.n_ins += 1
        if eng is self.PE:
            self.n_pe += 1
        if inc:
            eng.count += 1
            ins.then_inc(eng.sem, 1)
            val = eng.count
        else:
            val = eng.count + 1
        tag = (eng.name, val, eng)
        for b in R:
            old = b.r.get(eng.name)
            if old is None or old[1] < val:
                b.r[eng.name] = tag
        for b in W:
            b.w = tag
            b.r = {}
        return ins

    def dma(self, q, out, in_, R, W, ds):
        self._sync(q, R, W)
        k = "d:" + ds.name
        if ds.count > 0 and q.waited.get(k, 0) < ds.count:
            q.e.wait_ge(ds.sem, ds.count)
            q.waited[k] = ds.count
        ins = q.e.dma_start(out=out, in_=in_)
        ds.count += 16
        ins.then_inc(ds.sem, 16)
        self.n_ins += 1
        tag = (k, ds.count, None)
        for b in R:
            b.r[k] = tag
        for b in W:
            b.w = tag
            b.r = {}

    def barrier(self, dsems=()):
        engs = (self.PE, self.ACT, self.DVE)
        for e in engs + (self.POOL,):
            for e2 in engs:
                if e2 is e or e2.count == 0:
                    continue
                if e.waited.get(e2.name, 0) < e2.count:
                    e.e.wait_ge(e2.sem, e2.count)
                    e.waited[e2.name] = e2.count
            if e is self.POOL:
                continue
            for d in dsems:
                k = "d:" + d.name
                if d.count and e.waited.get(k, 0) < d.count:
                    e.e.wait_ge(d.sem, d.count)
                    e.waited[k] = d.count

    def mm(self, out, lhsT, rhs, R, W, start=True, stop=True, inc=None, **kw):
        if inc is None:
            inc = stop
        return self.op(self.PE, lambda: self.nc.tensor.matmul(out, lhsT, rhs, start=start, stop=stop, **kw), R, W, inc)

    def tr(self, out, in_, ident, R, W, inc=True):
        return self.op(self.PE, lambda: self.nc.tensor.transpose(out, in_, ident), R, W, inc)

    def act(self, out, in_, func, R, W, **kw):
        return self.op(self.ACT, lambda: self.nc.scalar.activation(out=out, in_=in_, func=func, **kw), R, W)

    def tt(self, out, in0, in1, op, R, W):
        return self.op(self.DVE, lambda: self.nc.vector.tensor_tensor(out=out, in0=in0, in1=in1, op=op), R, W)

    def ts(self, out, in0, s1, s2, op0, op1, R, W):
        if op1 is None:
            return self.op(self.DVE, lambda: self.nc.vector.tensor_scalar(out=out, in0=in0, scalar1=s1, scalar2=None, op0=op0), R, W)
        return self.op(self.DVE, lambda: self.nc.vector.tensor_scalar(out=out, in0=in0, scalar1=s1, scalar2=s2, op0=op0, op1=op1), R, W)

    def stt(self, out, in0, scalar, in1, op0, op1, R, W):
        return self.op(self.DVE, lambda: self.nc.vector.scalar_tensor_tensor(out=out, in0=in0, scalar=scalar, in1=in1, op0=op0, op1=op1), R, W)

    def ptt(self, out, in0, in1, op, R, W):
        return self.op(self.POOL, lambda: self.nc.gpsimd.tensor_tensor(out=out, in0=in0, in1=in1, op=op), R, W)

    def recip(self, out, in_, R, W):
        return self.op(self.DVE, lambda: self.nc.vector.reciprocal(out=out, in_=in_), R, W)

    def vcopy(self, out, in_, R, W):
        return self.op(self.DVE, lambda: self.nc.vector.tensor_copy(out=out, in_=in_), R, W)


def build_program(nseq=2, nt=8, dbg=None):
    S = nt * T
    nc = bass.Bass("TRN2", target_bir_lowering=False)
    dt_in = lambda name, shape: nc.dram_tensor(name, shape, F32, kind="ExternalInput").ap()
    x_d = dt_in("x", [nseq, S, D])
    mem_d = dt_in("mem", [nseq, 256, D])
    wall_d = dt_in("wall", [NBLK, 128, 4096])
    gcols_d = dt_in("gcols", [128, 96])
    gpost_d = dt_in("gpost", [3, 128, D])
    cmask_d = dt_in("cmask", [128, 4 * NSLOT_AT * 128 + 512 + 512 + 256])
    out_d = nc.dram_tensor("out", [nseq, S, D], F32, kind="ExternalOutput").ap()
    wb_d = nc.dram_tensor("wb", [NBLK, 128, 4096], BF16, kind="Internal").ap()
    dbg_out = {}

    es = ExitStack()
    with es:
        c = Ctx(nc, es)
        PE, ACT, DVE, POOL, SP = c.PE, c.ACT, c.DVE, c.POOL, c.SP

        uniq = {"n": 0}

        def sb(name, shape, dt, stack=es):
            uniq["n"] += 1
            return stack.enter_context(nc.sbuf_tensor(f"sb{uniq['n']}_{name}", shape, dt))

        ident = sb("ident", [128, 128], BF16)
        ones = sb("ones", [128, 128], BF16)
        zeros = sb("zeros", [128, 128], BF16)
        amask = sb("amask", [128, 4, NSLOT_AT, 128], BF16)
        hmask = sb("hmask", [128, 512], BF16)
        rmask = sb("rmask", [128, 512], F32)
        gcols = sb("gcols", [128, 96], F32)
        lbc = sb("lbc", [128, 16], F32)
        kT0 = sb("kT0", [128, 4, 5, 128], BF16)
        v0 = sb("v0", [128, 5, 512], BF16)
        kT1 = sb("kT1", [128, 4, 2, 512], BF16)
        v1 = sb("v1", [128, 2, 4, 512], BF16)
        kT2 = sb("kT2", [128, 4, 5, 512], BF16)
        v2 = sb("v2", [128, 5, 4, 512], BF16)
        Sst = sb("Sst", [128, 4, 128], F32)
        mkT = sb("mkT", [128, 4, 256], BF16)
        mvT = sb("mvT", [128, 2, 512], BF16)
        h = sb("h", [128, 4, D], F32)
        xT = sb("xT", [128, 8, T], BF16)
        wring = sb("wring", [128, SLOTS, 4096], BF16)
        ybr_box = {}
        stat = sb("stat", [128, 16], F32)
        PS = es.enter_context(nc.psum_tensor("PS", [128, 8, 512], F32))

        B = Buf
        b_const = B("const")
        b_h = [B(f"h{i}") for i in range(4)]
        b_xT = B("xT")
        b_ps = [B(f"ps{i}") for i in range(8)]
        b_slot = [B(f"slot{i}") for i in range(SLOTS)]
        b_kv = {n: B(n) for n in ("k0", "v0", "k1", "v1", "k2", "v2", "S", "mk", "mv", "lb")}
        b_ybr = [B(f"ybr{i}") for i in range(3)]
        b_stat = [B(f"stat{i}") for i in range(16)]

        d_cv = [c.dsem(f"cv{i}") for i in range(8)]
        d_const = c.dsem("cst")
        d_slot = [c.dsem(f"w{i}") for i in range(SLOTS)]
        d_x = [c.dsem(f"x{i}") for i in range(4)]
        d_o = [c.dsem(f"o{i}") for i in range(4)]
        d_misc = c.dsem("misc")
        d_dbg = c.dsem("dbg")
        BDS = (d_dbg, d_misc) + tuple(d_o)

        NM = 4 * NSLOT_AT * 128
        c.dma(POOL, amask[:].rearrange("p a b c -> p (a b c)"), cmask_d[:, 0:NM], [], [b_const], d_const)
        c.dma(POOL, hmask[:], cmask_d[:, NM:NM + 512], [], [b_const], d_const)
        c.dma(POOL, rmask[:], cmask_d[:, NM + 512:NM + 1024], [], [b_const], d_const)
        c.dma(POOL, ident[:], cmask_d[:, NM + 1024:NM + 1152], [], [b_const], d_const)
        c.dma(POOL, ones[:], cmask_d[:, NM + 1152:NM + 1280], [], [b_const], d_const)
        c.dma(POOL, gcols[:], gcols_d[:, :], [], [b_const], d_const)
        b_wbk = [B(f"wb{i}") for i in range(NBLK)]
        cstate = {"ptr": 0, "order": None}

        def conv_upto(n):
            order = cstate["order"]
            while cstate["ptr"] < min(n, len(order)):
                j = cstate["ptr"]
                blk = order[j]
                c.dma(POOL, wb_d[blk], wall_d[blk], [], [b_wbk[blk]], d_cv[j % 8])
                cstate["ptr"] += 1
        c.op(DVE, lambda: nc.vector.memset(zeros[:], 0.0), [], [b_const])
        c.tt(lbc[:, 8:12], gcols[:, 40:44], gcols[:, 44:48], ALU.subtract, [b_const], [b_kv["lb"]])
        c.act(lbc[:, 0:4], lbc[:, 8:12], AF.Sigmoid, [b_kv["lb"]], [b_kv["lb"]])
        c.act(lbc[:, 4:8], lbc[:, 8:12], AF.Sigmoid, [b_kv["lb"]], [b_kv["lb"]], scale=-1.0)
        c.act(lbc[:, 12:16], lbc[:, 4:8], AF.Copy, [b_kv["lb"]], [b_kv["lb"]], scale=-1.0)

        G_FFN1, G_MIX, G_FFN2, G_MEM = 0, 8, 16, 24

        wstate = {"issued": 0, "cur": 0, "seq": []}

        def w_issue():
            i = wstate["issued"]
            if i >= len(wstate["seq"]):
                return
            blk = wstate["seq"][i]
            s = i % SLOTS
            conv_upto(first_use[blk] + 9)
            c.dma(SP, wring[:, s, :], wb_d[blk], [b_wbk[blk]], [b_slot[s]], d_slot[s])
            wstate["issued"] += 1

        def w_get(expect, ahead=0):
            i = wstate["cur"] + ahead
            assert wstate["seq"][i] == expect, (wstate["seq"][i], expect)
            while wstate["issued"] <= i:
                w_issue()
            s = i % SLOTS
            return wring[:, s, :], b_slot[s]

        def w_done():
            wstate["cur"] += 1
            while wstate["issued"] < min(wstate["cur"] + SLOTS, len(wstate["seq"])):
                w_issue()

        tile_order = list(range(17)) + [17, 18, 19, 20] + list(range(21, 30)) + [30]
        merge_order = []
        for half in range(2):
            for b in range(3):
                merge_order += [31 + 2 * b + half, 37 + b]
        tile_order += merge_order + [40, 41] + list(range(42, 59))
        STAGE = build_program.stage
        if STAGE == 1:
            tile_order = list(range(17))
        elif STAGE == 2:
            tile_order = tile_order[:-17]
        for s_ in range(nseq):
            if STAGE >= 2:
                wstate["seq"] += [59, 60]
            for t_ in range(nt):
                wstate["seq"] += tile_order

        cstate["order"] = list(dict.fromkeys(wstate["seq"]))
        assert sorted(cstate["order"]) == list(range(NBLK))
        first_use = {blk: j for j, blk in enumerate(cstate["order"])}

        def dump(name, ap, shape, R, dt=F32):
            if dbg is None or name not in dbg:
                return
            d = nc.dram_tensor("dbg_" + name, shape, dt, kind="ExternalOutput").ap()
            c.dma(SP, d, ap, R, [], d_dbg)
            dbg_out[name] = True

        psrot = {"i": 0}

        def ps_next(pool=(0, 1, 2, 3)):
            i = pool[psrot["i"] % len(pool)]
            psrot["i"] += 1
            return i

        def prenorm(srcs, gcol, dst, dstB, dst_off, stack):
            junk = sb("pn_junk", [128, D], BF16, stack)
            xn = sb("pn_xn", [128, 2, D], BF16, stack)
            b_junk, b_xn = B("junk"), [B("xn0"), B("xn1")]
            for i, (src, sB) in enumerate(srcs):
                st = b_stat[i % 4]
                col = (i % 4) * 3
                c.act(junk[:], src, AF.Square, [sB], [b_junk, st], accum_out=stat[:, col:col + 1])
                c.act(stat[:, col + 1:col + 2], stat[:, col:col + 1], AF.Sqrt, [st, b_const], [st], scale=1.0 / D, bias=epsb[:, 0:1])
                c.recip(stat[:, col + 2:col + 3], stat[:, col + 1:col + 2], [st], [st])
                c.act(xn[:, i % 2, :], src, AF.Copy, [sB, st], [b_xn[i % 2]], scale=stat[:, col + 2:col + 3])
                pi = ps_next()
                pst = PS[:, pi, :].bitcast(BF16)
                for dc in range(8):
                    c.tr(pst[:, dc * 128:(dc + 1) * 128], xn[:, i % 2, dc * 128:(dc + 1) * 128], ident[:],
                         [b_xn[i % 2], b_const], [b_ps[pi]], inc=(dc == 7))
                c.tt(dst[:, :, dst_off + i * 128:dst_off + (i + 1) * 128],
                     pst.rearrange("p (c t) -> p c t", c=8),
                     gcols[:, gcol:gcol + 8].unsqueeze(2).to_broadcast([128, 8, 128]),
                     ALU.mult, [b_ps[pi], b_const], [dstB])

        def postnorm_residual(tc, ysb, b_y, gp, b_gp, coef):
            st = b_stat[4 + tc]
            col = 12 + 0
            col = tc * 3
            st2 = stat2
            c.act(ph["junk2"][:], ysb, AF.Square, [b_y], [b_junk2, st], accum_out=st2[:, col:col + 1])
            k = 1.0 / (coef * coef)
            ec = 1 if coef == 0.5 else 0
            c.act(st2[:, col + 1:col + 2], st2[:, col:col + 1], AF.Sqrt, [st, b_const], [st], scale=k / D, bias=epsb[:, ec:ec + 1])
            c.recip(st2[:, col + 2:col + 3], st2[:, col + 1:col + 2], [st], [st])
            c.tt(ysb, ysb, gp, ALU.mult, [b_y, b_gp], [b_y])
            c.stt(h[:, tc, :], ysb, st2[:, col + 2:col + 3], h[:, tc, :], ALU.mult, ALU.add, [b_y, st, b_h[tc]], [b_h[tc]])

        epsb = sb("epsb", [128, 2], F32)
        c.op(DVE, lambda: nc.vector.memset(epsb[:, 0:1], EPS), [], [b_const])
        c.op(DVE, lambda: nc.vector.memset(epsb[:, 1:2], 4.0 * EPS), [], [b_const])
        stat2 = sb("stat2", [128, 12], F32)
        b_junk2 = B("junk2")
        ph = {}

        def load_gpost(idx, stack):
            gp = sb("gpost", [128, D], F32, stack)
            bg = B("gpost")
            c.dma(POOL, gp[:], gpost_d[idx], [], [bg], d_misc)
            return gp, bg

        def ffn_alloc(stack):
            fb = {"actT": sb("actT", [128, NFC, T], BF16, stack), "sg": sb("ffn_sg", [128, 2, T], F32, stack),
                  "b_act": [B(f"act{j}") for j in range(NFC)], "b_sg": [B("sg0"), B("sg1")]}
            alloc_ystage(stack)
            fb["gp"] = sb("gpost", [128, D], F32, stack)
            fb["b_gp"] = B("gpost")
            return fb

        def ffn(base, gidx, nxt, stack, fb=None, mid_hook=None):
            if fb is None:
                fb = ffn_alloc(stack)
            actT, sg, b_act, b_sg = fb["actT"], fb["sg"], fb["b_act"], fb["b_sg"]
            gp, b_gp = fb["gp"], fb["b_gp"]
            c.dma(POOL, gp[:], gpost_d[gidx], [], [b_gp], d_misc)
            for blk in range(11):
                w, wB = w_get(base + blk)
                w3 = w.rearrange("p (c n) -> p c n", c=8)
                for pi_ in range(2):
                    j = 2 * blk + pi_
                    pg, pu = ps_next(), ps_next()
                    for dc in range(8):
                        c.mm(PS[:, pg, :], w3[:, dc, pi_ * 256:pi_ * 256 + 128], xT[:, dc, :], [wB, b_xT], [b_ps[pg]],
                             start=(dc == 0), stop=(dc == 7))
                    for dc in range(8):
                        c.mm(PS[:, pu, :], w3[:, dc, pi_ * 256 + 128:pi_ * 256 + 256], xT[:, dc, :], [wB, b_xT], [b_ps[pu]],
                             start=(dc == 0), stop=(dc == 7))
                    c.act(sg[:, j % 2, :], PS[:, pg, :], AF.Silu, [b_ps[pg]], [b_sg[j % 2]])
                    c.tt(actT[:, j, :], sg[:, j % 2, :], PS[:, pu, :], ALU.mult, [b_sg[j % 2], b_ps[pu]], [b_act[j]])
                w_done()
            preload_sqrt_table()
            for ch in range(2):
                banks = (4, 5, 6, 7) if ch == 0 else (0, 1, 2, 3)
                for blk in range(3):
                    w, wB = w_get(base + 11 + ch * 3 + blk)
                    w3 = w.rearrange("p (c n) -> p c n", c=8)
                    nf = 8 if blk < 2 else 6
                    for fi in range(nf):
                        f = blk * 8 + fi
                        for tc in range(4):
                            c.mm(PS[:, banks[tc], :], actT[:, f, tc * 128:(tc + 1) * 128], w3[:, fi, :], [b_act[f], wB], [b_ps[banks[tc]]],
                                 start=(f == 0), stop=(f == NFC - 1), inc=(f == NFC - 1 or (fi == nf - 1 and tc == 3)))
                    w_done()
                    if mid_hook is not None and ch == 0 and blk == 1:
                        mid_hook()
                for tc in range(4):
                    evac_y(tc, ch, banks[tc], gp, b_gp)
            c.mark("tail")
            sublayer_tail(0.5, nxt)

        stats = sb("stats", [128, 32], F32)
        b_ssq = [B(f"ssq{i}") for i in range(4)]
        b_ss2 = [B(f"ss2{i}") for i in range(4)]
        b_sA, b_sB = B("sA"), B("sB")
        b_jq = [B("jq0"), B("jq1")]
        b_xn4 = [B(f"xn4{i}") for i in range(4)]
        njq = [0]
        dummy = sb("dummy", [128, 2], F32)
        b_dummy = B("dummy")

        def preload_sqrt_table():
            c.act(dummy[:, 0:1], epsb[:, 0:1], AF.Sqrt, [b_const], [b_dummy])

        def evac_y(tc, ch, bank, gp, b_gp):
            ys = ph["ystage"]
            c.tt(ys[:, tc, ch * 512:(ch + 1) * 512], PS[:, bank, :], gp[:, ch * 512:(ch + 1) * 512], ALU.mult,
                 [b_ps[bank], b_gp], [b_ystage[tc]])
            i2 = njq[0] % 2
            njq[0] += 1
            c.act(ph["junkq"][:, i2, 0:512], PS[:, bank, :], AF.Square, [b_ps[bank], b_ystage[tc]], [b_jq[i2], b_ssq[tc]],
                  accum_out=stats[:, tc * 2 + ch:tc * 2 + ch + 1])

        def sublayer_tail(coef, nxt):
            ys = ph["ystage"]
            k = 1.0 / (coef * coef)
            ec = 1 if coef == 0.5 else 0
            sv = stats[:, 0:8].rearrange("p (t c) -> p t c", c=2)
            c.tt(stats[:, 8:12], sv[:, :, 0], sv[:, :, 1], ALU.add, b_ssq, [b_sA])
            c.act(stats[:, 12:16], stats[:, 8:12], AF.Sqrt, [b_sA, b_const], [b_sA], scale=k / D, bias=epsb[:, ec:ec + 1])
            c.recip(stats[:, 16:20], stats[:, 12:16], [b_sA], [b_sA])
            final = nxt.get("final", False)
            resid, b_resid = nxt.get("resid", (h, b_h))
            for tc in range(4):
                if not final:
                    c.stt(h[:, tc, :], ys[:, tc, :], stats[:, 16 + tc:17 + tc], resid[:, tc, :], ALU.mult, ALU.add,
                          [b_ystage[tc], b_sA, b_resid[tc]] + ([b_h[tc]] if b_resid is not b_h else []), [b_h[tc]])
                else:
                    seq, ti = nxt["seq"], nxt["ti"]
                    t0 = ti * T
                    c.stt(ys[:, tc, :], ys[:, tc, :], stats[:, 16 + tc:17 + tc], h[:, tc, :], ALU.mult, ALU.add,
                          [b_ystage[tc], b_sA, b_h[tc]], [b_ystage[tc]])
                    c.dma(POOL, out_d[seq, t0 + tc * 128:t0 + (tc + 1) * 128, :], ys[:, tc, :], [b_ystage[tc]], [], d_o[tc])
                    if nxt.get("gcol") is not None:
                        t1 = t0 + T
                        c.dma(POOL, h[:, tc, :], x_d[seq, t1 + tc * 128:t1 + (tc + 1) * 128, :], [], [b_h[tc]], d_x[tc])
            gcol = nxt.get("gcol")
            if gcol is None:
                return
            tail_pre(gcol, h, b_h)

        def tail_pre(gcol, h, b_h):
            xn4 = ph["xn4"]
            for tc in range(4):
                c.act(xn4[:, tc, :], h[:, tc, :], AF.Square, [b_h[tc]], [b_xn4[tc], b_ss2[tc]],
                      accum_out=stats[:, 20 + tc:21 + tc])
            c.act(stats[:, 24:28], stats[:, 20:24], AF.Sqrt, b_ss2 + [b_const], [b_sB], scale=1.0 / D, bias=epsb[:, 0:1])
            c.recip(stats[:, 28:32], stats[:, 24:28], [b_sB], [b_sB])
            for tc in range(4):
                if tc < 2:
                    c.act(xn4[:, tc, :], h[:, tc, :], AF.Copy, [b_h[tc], b_sB], [b_xn4[tc]], scale=stats[:, 28 + tc:29 + tc])
                elif True:
                    c.ts(xn4[:, tc, :], h[:, tc, :], stats[:, 28 + tc:29 + tc], None, ALU.mult, None, [b_h[tc], b_sB], [b_xn4[tc]])
                else:
                    c.op(POOL, lambda tc=tc: nc.gpsimd.tensor_scalar(out=xn4[:, tc, :], in0=h[:, tc, :], scalar1=stats[:, 28 + tc:29 + tc],
                                                                      scalar2=None, op0=ALU.mult), [b_h[tc], b_sB], [b_xn4[tc]])
            for tc in range(4):
                pi = ps_next()
                pst = PS[:, pi, :].bitcast(BF16)
                for dc in range(8):
                    c.tr(pst[:, dc * 128:(dc + 1) * 128], xn4[:, tc, dc * 128:(dc + 1) * 128], ident[:],
                         [b_xn4[tc], b_const], [b_ps[pi]], inc=(dc == 7))
                c.tt(xT[:, :, tc * 128:(tc + 1) * 128], pst.rearrange("p (c t) -> p c t", c=8),
                     gcols[:, gcol:gcol + 8].unsqueeze(2).to_broadcast([128, 8, 128]), ALU.mult, [b_ps[pi], b_const], [b_xT])

        b_ystage = [B(f"ys{i}") for i in range(4)]

        def alloc_ystage(stack):
            ph["ystage"] = sb("ystage", [128, 4, D], F32, stack)
            ph["junkq"] = sb("junkq", [128, 2, 512], BF16, stack)
            ph["xn4"] = sb("xn4", [128, 4, D], BF16, stack)

        def ysbuf(tc):
            return ph["ystage"][:, tc, :], b_ystage[tc]

        b_xnext = [B(f"xnext{i}") for i in range(4)]

        def tile(seq, ti):
            if ti == 0:
                c.mark(f"t{ti}.ffn1")
                with ExitStack() as st:
                    ffn(0, 0, {"gcol": G_MIX}, st)
                c.barrier(BDS)
            c.mark(f"t{ti}.mixer")
            mixer(seq, ti)
            c.barrier(BDS)
            c.mark(f"t{ti}.ffn2")
            has_next = ti + 1 < nt
            with ExitStack() as st:
                fb = ffn_alloc(st)
                hook = None
                if has_next:
                    xnext = sb("xnext", [128, 4, D], F32, st)
                    t1 = (ti + 1) * T
                    for tc in range(4):
                        c.dma(POOL, xnext[:, tc, :], x_d[seq, t1 + tc * 128:t1 + (tc + 1) * 128, :], [], [b_xnext[tc]], d_x[tc])
                    hook = lambda: tail_pre(G_FFN1, xnext, b_xnext)
                ffn(42, 2, {"final": True, "seq": seq, "ti": ti, "gcol": None}, st, fb=fb, mid_hook=hook)
                if has_next:
                    c.mark(f"t{ti + 1}.ffn1")
                    ffn(0, 0, {"gcol": G_MIX, "resid": (xnext, b_xnext)}, st, fb=fb)
            c.barrier(BDS)
            c.mark(f"t{ti}.end")

        def first_prenorm(seq):
            for tc in range(4):
                c.dma(POOL, h[:, tc, :], x_d[seq, tc * 128:(tc + 1) * 128, :], [], [b_h[tc]], d_x[tc])
            with ExitStack() as st:
                prenorm([(h[:, tc, :], b_h[tc]) for tc in range(4)], G_FFN1, xT, b_xT, 0, st)
            c.barrier(BDS)

        ppool = {"p": (0, 1, 2, 3)}

        def proj_fm(w3, col0, dst_fn, evac):
            p = ps_next(ppool["p"])
            for dc in range(8):
                c.mm(PS[:, p, :], w3[:, dc, col0:col0 + 128], xT[:, dc, :], [dst_fn, b_xT], [b_ps[p]], start=(dc == 0), stop=(dc == 7))
            evac(p)

        def proj_tm(w3, wB, lhs_fn, evac):
            p = ps_next(ppool["p"])
            for dc in range(8):
                c.mm(PS[:, p, :], lhs_fn(dc), w3[:, dc, :], [wB, b_xT], [b_ps[p]], start=(dc == 0), stop=(dc == 7))
            evac(p)

        def seq_prologue(seq):
            c.op(DVE, lambda: nc.vector.memset(Sst[:].rearrange("p a b -> p (a b)"), 0.0), [], [b_kv["S"]])
            with ExitStack() as st:
                mtmp = sb("memtmp", [128, 2, D], F32, st)
                memT = sb("memT", [128, 8, 256], BF16, st)
                b_mt, b_memT = [B("mt0"), B("mt1")], B("memT")
                for mc in range(2):
                    c.dma(POOL, mtmp[:, mc, :], mem_d[seq, mc * 128:(mc + 1) * 128, :], [], [b_mt[mc]], d_x[mc])
                prenorm([(mtmp[:, mc, :], b_mt[mc]) for mc in range(2)], G_MEM, memT, b_memT, 0, st)
                w, wB = w_get(59)
                w3 = w.rearrange("p (c n) -> p c n", c=8)
                for hh in range(4):
                    p = ps_next()
                    for dc in range(8):
                        c.mm(PS[:, p, 0:256], w3[:, dc, hh * 128:(hh + 1) * 128], memT[:, dc, :], [wB, b_memT], [b_ps[p]],
                             start=(dc == 0), stop=(dc == 7))
                    c.act(mkT[:, hh, :], PS[:, p, 0:256], AF.Copy, [b_ps[p]], [b_kv["mk"]])
                w_done()
                w, wB = w_get(60)
                w3 = w.rearrange("p (c n) -> p c n", c=8)
                for mc in range(2):
                    p = ps_next()
                    for dc in range(8):
                        c.mm(PS[:, p, :], memT[:, dc, mc * 128:(mc + 1) * 128], w3[:, dc, :], [wB, b_memT], [b_ps[p]],
                             start=(dc == 0), stop=(dc == 7))
                    c.act(mvT[:, mc, :], PS[:, p, :], AF.Copy, [b_ps[p]], [b_kv["mv"]])
                w_done()
            c.barrier(BDS)

        def hgrn(ti, st, pgen):
            qs = sb("hg_qs", [128, 4, T], BF16, st)
            sgf = sb("hg_sg", [128, 4, T], F32, st)
            vhg = sb("hg_v", [128, 4, T], BF16, st)
            sgo = sb("hg_sgo", [128, 4, T], BF16, st)
            b_qs, b_sgf, b_vhg, b_sgo = B("qs"), B("sgf"), B("vhg"), B("sgo")
            for bi, (dst, dB, fn) in enumerate(((qs, b_qs, AF.Silu), (sgf, b_sgf, AF.Sigmoid), (None, None, None), (sgo, b_sgo, AF.Sigmoid))):
                w, wB = w_get(17 + bi)
                w3 = w.rearrange("p (c n) -> p c n", c=8)
                if dst is not None:
                    for hh in range(4):
                        proj_fm(w3, hh * 128, wB, lambda p, hh=hh: c.act(dst[:, hh, :], PS[:, p, :], fn, [b_ps[p]], [dB]))
                else:
                    for tc in range(4):
                        proj_tm(w3, wB, lambda dc, tc=tc: xT[:, dc, tc * 128:(tc + 1) * 128],
                                lambda p, tc=tc: c.act(vhg[:, tc, :], PS[:, p, :], AF.Copy, [b_ps[p]], [b_vhg]))
                w_done()
            lf = sb("hg_lf", [128, T], F32, st)
            bb = sb("hg_b", [128, T], F32, st)
            bm = sb("hg_bm", [128, T], F32, st)
            E1 = sb("hg_E1", [128, T], F32, st)
            qe = sb("hg_qe", [128, 2, T], BF16, st)
            ke = sb("hg_ke", [128, 2, T], BF16, st)
            keTM = sb("hg_keTM", [128, 2, T], BF16, st)
            scT = sb("hg_scT", [128, 2, T], BF16, st)
            dece = sb("hg_dec", [128, 2, 24], F32, st)
            Sp = sb("hg_Sp", [128, 1, 8, 128], BF16, st)
            tkv = sb("hg_tkv", [128, 2, 8, 128], F32, st)
            bl, bbB, bbm, bE1 = [B(n) for n in ("lf", "bb", "bm", "E1")]
            E2, bE2 = bm, bbm
            kk, bkk = lf, bl
            rt, y1 = E1, E2
            bqe, bke, bkeTM, bscT, bdec, bSp, btkv = [[B(n + "0"), B(n + "1")] for n in ("qe", "ke", "keTM", "scT", "dec", "Sp", "tkv")]
            bosq, brt, by1 = bl, bE1, bE2
            osq_v = lf[:].bitcast(BF16)[:, 0:T]
            b3 = bb[:].rearrange("p (c t) -> p c t", t=64)
            bm3 = bm[:].rearrange("p (c t) -> p c t", t=64)
            PKV = (2, 3)

            def stageA(hh):
                i2 = hh % 2
                oml = lbc[:, 4 + hh:5 + hh]
                lbh = lbc[:, hh:hh + 1]
                c.act(lf[:], sgf[:, hh, :], AF.Ln, [b_sgf, b_kv["lb"]], [bl], scale=oml, bias=lbh)
                c.op(DVE, lambda: nc.vector.tensor_tensor_scan(out=bb[:], data0=rmask[:], data1=lf[:], initial=0.0,
                                                               op0=ALU.mult, op1=ALU.add), [bl, b_const], [bbB])
                c.tt(bm3, b3, b3[:, :, 31:32].to_broadcast([128, 8, 64]), ALU.subtract, [bbB], [bbm])
                c.act(E1[:], bm[:], AF.Exp, [bbm], [bE1])
                c.act(dece[:, i2, 16:24], bm3[:, :, 63], AF.Exp, [bbm], [bdec[i2]])
                c.act(E2[:], bm[:], AF.Exp, [bbm], [bE2], scale=-1.0)
                c.act(dece[:, i2, 0:8], b3[:, :, 63], AF.Exp, [bbB], [bdec[i2]])
                c.act(dece[:, i2, 8:16], b3[:, :, 31], AF.Exp, [bbB], [bdec[i2]])
                c.tt(qe[:, i2, :], qs[:, hh, :], E1[:], ALU.mult, [b_qs, bE1], [bqe[i2]])
                c.act(kk[:], sgf[:, hh, :], AF.Identity, [b_sgf, b_kv["lb"]], [bkk], scale=lbc[:, 12 + hh:13 + hh], bias=oml)
                c.tt(ke[:, i2, :], kk[:], E2[:], ALU.mult, [bkk, bE2], [bke[i2]])

            def P(n):
                for _ in range(n):
                    next(pgen, None)

            def stageAp(hh):
                i2 = hh % 2
                P(2)
                p = ps_next((0, 1))
                pst = PS[:, p, :].bitcast(BF16)
                for j in range(4):
                    c.tr(pst[:, j * 128:(j + 1) * 128], ke[:, i2, j * 128:(j + 1) * 128], ident[:], [bke[i2], b_const], [b_ps[p]], inc=(j == 3))
                c.act(keTM[:, i2, :], pst[:, 0:512], AF.Copy, [b_ps[p]], [bkeTM[i2]])
                p = ps_next((0, 1))
                for j in range(4):
                    c.mm(PS[:, p, j * 128:(j + 1) * 128], ke[:, i2, j * 128:(j + 1) * 128], qe[:, i2, j * 128:(j + 1) * 128],
                         [bke[i2], bqe[i2]], [b_ps[p]], start=True, stop=True, inc=(j == 3))
                c.tt(scT[:, i2, :], PS[:, p, :], hmask[:], ALU.mult, [b_ps[p], b_const], [bscT[i2]])
                P(2)
                for cc in range(8):
                    j, par = cc // 2, cc % 2
                    lo = par * 64
                    c.mm(PS[:, PKV[par], j * 128:(j + 1) * 128], keTM[lo:lo + 64, i2, j * 128:(j + 1) * 128],
                         vhg[lo:lo + 64, j, hh * 128:(hh + 1) * 128],
                         [bkeTM[i2], b_vhg], [b_ps[PKV[par]]], start=True, stop=True, inc=(cc >= 6))
                for par in range(2):
                    c.tt(tkv[:, i2, :, :].rearrange("p (j r) v -> p r j v", r=2)[:, par],
                         PS[:, PKV[par], :].rearrange("p (j v) -> p j v", v=128),
                         dece[:, i2, 16:24].rearrange("p (j r) -> p r j", r=2)[:, par].unsqueeze(2).to_broadcast([128, 4, 128]),
                         ALU.mult, [b_ps[PKV[par]], bdec[i2]], [btkv[i2]])

            def stageB(hh):
                i2 = hh % 2
                if build_program.skipB:
                    return
                for cc in range(8):
                    c.ts(Sp[:, 0, cc, :], Sst[:, hh, :], dece[:, i2, 8 + cc:9 + cc], None, ALU.mult, None, [b_kv["S"], bdec[i2]], [bSp[0]])
                    c.stt(Sst[:, hh, :], Sst[:, hh, :], dece[:, i2, cc:cc + 1], tkv[:, i2, cc, :], ALU.mult, ALU.add,
                          [b_kv["S"], bdec[i2], btkv[i2]], [b_kv["S"]])

            def stageBp(hh):
                i2 = hh % 2
                P(3)
                po = 4 + (hh % 2)
                for j in range(4):
                    c.mm(PS[:, po, j * 128:(j + 1) * 128], vhg[:, j, hh * 128:(hh + 1) * 128], scT[:, i2, j * 128:(j + 1) * 128],
                         [b_vhg, bscT[i2]], [b_ps[po]], start=True, stop=False, inc=False, skip_group_check=True)
                    for cc in (2 * j, 2 * j + 1):
                        c.mm(PS[:, po, cc * 64:(cc + 1) * 64], Sp[:, 0, cc, :], qe[:, i2, cc * 64:(cc + 1) * 64], [bSp[0], bqe[i2]], [b_ps[po]],
                             start=False, stop=True, inc=(cc == 7), skip_group_check=True)
                c.act(osq_v, PS[:, po, :], AF.Square, [b_ps[po]], [bosq])
                P(3)
                pn = ps_next((0, 1))
                c.mm(PS[:, pn, :], ones[:], osq_v, [b_const, bosq], [b_ps[pn]], start=True, stop=True)
                c.act(rt[:], PS[:, pn, :], AF.Ln, [b_ps[pn], b_const], [brt], scale=1.0 / 128, bias=epsb[:, 0:1])
                c.act(rt[:], rt[:], AF.Exp, [brt], [brt], scale=-0.5)
                c.stt(y1[:], PS[:, po, :], gcols[:, 32:33], rt[:], ALU.mult, ALU.mult, [b_ps[po], brt, b_const], [by1])
                c.tt(ybr_box[0][:, hh, :], y1[:], sgo[:, hh, :], ALU.mult, [by1, b_sgo], [b_ybr[0]])

            c.mark("hgrn_stages")
            ppool["p"] = (6, 7)
            for fn, a in ((stageA, 0), (stageAp, 0), (stageA, 1), (stageAp, 1), (stageB, 0), (stageBp, 0),
                          (stageA, 2), (stageAp, 2), (stageB, 1), (stageBp, 1), (stageA, 3), (stageAp, 3),
                          (stageB, 2), (stageBp, 2), (stageB, 3), (stageBp, 3)):
                fn(a)
            for _ in pgen:
                pass
            ppool["p"] = (0, 1, 2, 3)

        def attn_proj_gen(ti, actx):
            qg, mq = actx["qg"], actx["mq"]
            b_q = [B(f"q{g}") for g in range(3)]
            b_mq = B("mq")
            gc0 = ti * 4
            kcache = (kT0, kT1, kT2)
            kB = (b_kv["k0"], b_kv["k1"], b_kv["k2"])
            vB = (b_kv["v0"], b_kv["v1"], b_kv["v2"])
            nev = [0]
            actx.update(qg=qg, mq=mq, b_q=b_q, b_mq=b_mq, kB=kB, vB=vB, gc0=gc0)

            def evac(out, in_, R, W):
                nev[0] += 1
                if nev[0] % 4:
                    c.act(out, in_, AF.Copy, R, W)
                else:
                    c.vcopy(out, in_, R, W)

            def perm(ap):
                return ap.rearrange("p (s r) -> p r s", r=4)

            def unperm(ap):
                return ap.rearrange("p (r s) -> p r s", r=4)

            for g in range(3):
                w, wB = w_get(21 + g * 3)
                w3 = w.rearrange("p (c n) -> p c n", c=8)
                for hh in range(4):
                    if g == 0:
                        proj_fm(w3, hh * 128, wB, lambda p, hh=hh: evac(qg[0][:, hh, :], PS[:, p, :], [b_ps[p]], [b_q[0]]))
                        yield
                    else:
                        proj_fm(w3, hh * 128, wB, lambda p, hh=hh, g=g: evac(unperm(qg[g][:, hh, :]), perm(PS[:, p, :]), [b_ps[p]], [b_q[g]]))
                        yield
                w_done()
                w, wB = w_get(22 + g * 3)
                w3 = w.rearrange("p (c n) -> p c n", c=8)
                for hh in range(4):
                    if g == 0:
                        def ev(p, hh=hh):
                            for cq in range(4):
                                evac(kT0[:, hh, (gc0 + cq) % 5, :], PS[:, p, cq * 128:(cq + 1) * 128], [b_ps[p]], [kB[0]])
                        proj_fm(w3, hh * 128, wB, ev)
                        yield
                    else:
                        slot = ti % (2 if g == 1 else 5)
                        proj_fm(w3, hh * 128, wB, lambda p, hh=hh, g=g, slot=slot:
                                evac(unperm(kcache[g][:, hh, slot, :]), perm(PS[:, p, :]), [b_ps[p]], [kB[g]]))
                        yield
                w_done()
                w, wB = w_get(23 + g * 3)
                w3 = w.rearrange("p (c n) -> p c n", c=8)
                for tc in range(4):
                    if g == 0:
                        proj_tm(w3, wB, lambda dc, tc=tc: xT[:, dc, tc * 128:(tc + 1) * 128],
                                lambda p, tc=tc: evac(v0[:, (gc0 + tc) % 5, :], PS[:, p, :], [b_ps[p]], [vB[0]]))
                        yield
                    elif g == 1:
                        proj_tm(w3, wB, lambda dc, tc=tc: xT[:, dc, tc::4],
                                lambda p, tc=tc: evac(v1[:, ti % 2, tc, :], PS[:, p, :], [b_ps[p]], [vB[1]]))
                        yield
                    else:
                        proj_tm(w3, wB, lambda dc, tc=tc: xT[:, dc, tc::4],
                                lambda p, tc=tc: evac(v2[:, ti % 5, tc, :], PS[:, p, :], [b_ps[p]], [vB[2]]))
                        yield
                w_done()
            w, wB = w_get(30)
            w3 = w.rearrange("p (c n) -> p c n", c=8)
            for hh in range(4):
                proj_fm(w3, hh * 128, wB, lambda p, hh=hh: evac(mq[:, hh, :], PS[:, p, :], [b_ps[p]], [b_mq]))
                yield
            w_done()

        def attn(ti, st, actx):
            qg, mq, b_q, b_mq, kB, vB, gc0 = [actx[n] for n in ("qg", "mq", "b_q", "b_mq", "kB", "vB", "gc0")]

            def perm(ap):
                return ap.rearrange("p (s r) -> p r s", r=4)

            ebuf = sb("at_e", [128, 2, NSLOT_AT * 128], BF16, st)
            pbuf = sb("at_p", [128, 2, NSLOT_AT * 128], BF16, st)
            rden = sb("at_rden", [128, T], F32, st)
            b_e, b_p, b_rden = [B("e0"), B("e1")], [B("p0"), B("p1")], B("rden")
            it = [0]
            PN, PD = 6, 7
            numv = PS[:, PN, :]
            denv = PS[:, PD, :]
            scale = 128.0 ** -0.5
            items = []

            def dil_item(hh, qc, idx):
                blocks = []
                for dist, ms in ((4, 0), (3, 1), (2, 2), (1, 3)):
                    if ti - dist >= 0:
                        sl = (ti - dist) % 5
                        blocks.append((ms, kT2[:, hh, sl, qc * 128:(qc + 1) * 128], v2[:, sl, qc, hh * 128:(hh + 1) * 128], 2, False))
                if ti >= 1:
                    sl = (ti - 1) % 2
                    blocks.append((4, kT1[:, hh, sl, qc * 128:(qc + 1) * 128], v1[:, sl, qc, hh * 128:(hh + 1) * 128], 1, False))
                if gc0 + qc >= 1:
                    sl = (gc0 + qc - 1) % 5
                    blocks.append((5, kT0[:, hh, sl, :], v0[:, sl, hh * 128:(hh + 1) * 128], 0, True))
                sl = (gc0 + qc) % 5
                blocks.append((6, kT0[:, hh, sl, :], v0[:, sl, hh * 128:(hh + 1) * 128], 0, True))
                blocks.append((7, kT1[:, hh, ti % 2, qc * 128:(qc + 1) * 128], v1[:, ti % 2, qc, hh * 128:(hh + 1) * 128], 1, False))
                blocks.append((8, kT2[:, hh, ti % 5, qc * 128:(qc + 1) * 128], v2[:, ti % 5, qc, hh * 128:(hh + 1) * 128], 2, False))
                i2 = idx % 2
                sbase = 3 * i2
                scv = PS[:, sbase:sbase + 3, :].rearrange("p a b -> p (a b)")
                scB = [b_ps[sbase], b_ps[sbase + 1], b_ps[sbase + 2]]

                def A():
                    for bi, (ms, kap, vap, g, nat) in enumerate(blocks):
                        c.mm(scv[:, ms * 128:(ms + 1) * 128], kap, qg[g][:, hh, qc * 128:(qc + 1) * 128], [kB[g], b_q[g]], scB,
                             start=True, stop=True, inc=(bi == len(blocks) - 1))
                    lo = blocks[0][0] * 128
                    hi = NSLOT_AT * 128
                    c.act(ebuf[:, i2, lo:hi], scv[:, lo:hi], AF.Exp, scB, [b_e[i2]], scale=scale)
                    c.tt(pbuf[:, i2, lo:hi], ebuf[:, i2, lo:hi], amask[:, hh, :, :].rearrange("p a b -> p (a b)")[:, lo:hi], ALU.mult,
                         [b_e[i2], b_const], [b_p[i2]])

                def Bf():
                    if qc == 0:
                        c.mm(numv, zeros[:], qg[0][:, hh, :], [b_const, b_q[0]], [b_ps[PN]], start=True, stop=False, inc=False, skip_group_check=True)
                        c.mm(denv, zeros[:], qg[0][:, hh, :], [b_const, b_q[0]], [b_ps[PD]], start=True, stop=False, inc=False, skip_group_check=True)
                    for bi, (ms, kap, vap, g, nat) in enumerate(blocks):
                        if nat:
                            no, do = numv[:, qc * 128:(qc + 1) * 128], denv[:, qc * 128:(qc + 1) * 128]
                        else:
                            no, do = perm(numv)[:, qc, :], perm(denv)[:, qc, :]
                        last = (bi == len(blocks) - 1)
                        pv = pbuf[:, i2, ms * 128:(ms + 1) * 128]
                        c.mm(no, vap, pv, [vB[g], b_p[i2]], [b_ps[PN]], start=False, stop=last, inc=last, skip_group_check=True)
                        c.mm(do, ones[:], pv, [b_const, b_p[i2]], [b_ps[PD]], start=False, stop=last, inc=last, skip_group_check=True)
                    if qc == 3:
                        c.act(rden[:], denv, AF.Ln, [b_ps[PD]], [b_rden])
                        c.act(rden[:], rden[:], AF.Exp, [b_rden], [b_rden], scale=-1.0)
                        c.tt(ybr_box[1][:, hh, :], numv, rden[:], ALU.mult, [b_ps[PN], b_rden], [b_ybr[1]])
                return A, Bf

            def mem_item(hh, idx):
                i2 = idx % 2
                sbase = 3 * i2
                scv = PS[:, sbase:sbase + 3, :].rearrange("p a b -> p (a b)")
                scB = [b_ps[sbase], b_ps[sbase + 1], b_ps[sbase + 2]]

                def A():
                    for mc in range(2):
                        c.mm(scv[:, mc * 512:(mc + 1) * 512], mkT[:, hh, mc * 128:(mc + 1) * 128], mq[:, hh, :], [b_kv["mk"], b_mq], scB,
                             start=True, stop=True, inc=(mc == 1))
                    c.act(ebuf[:, i2, 0:1024], scv[:, 0:1024], AF.Exp, scB, [b_e[i2]], scale=scale)

                def Bf():
                    for mc in range(2):
                        c.mm(numv, mvT[:, mc, hh * 128:(hh + 1) * 128], ebuf[:, i2, mc * 512:(mc + 1) * 512], [b_kv["mv"], b_e[i2]], [b_ps[PN]],
                             start=(mc == 0), stop=(mc == 1))
                        c.mm(denv, ones[:], ebuf[:, i2, mc * 512:(mc + 1) * 512], [b_const, b_e[i2]], [b_ps[PD]],
                             start=(mc == 0), stop=(mc == 1))
                    c.act(rden[:], denv, AF.Ln, [b_ps[PD]], [b_rden])
                    c.act(rden[:], rden[:], AF.Exp, [b_rden], [b_rden], scale=-1.0)
                    c.tt(ybr_box[2][:, hh, :], numv, rden[:], ALU.mult, [b_ps[PN], b_rden], [b_ybr[2]])
                return A, Bf

            for hh in range(4):
                for qc in range(4):
                    items.append(dil_item(hh, qc, len(items)))
                items.append(mem_item(hh, len(items)))
            items[0][0]()
            for i in range(1, len(items)):
                items[i][0]()
                items[i - 1][1]()
            items[-1][1]()

        def merge(st):
            yT = sb("mg_yT", [128, 8, T], BF16, st)
            acc = sb("mg_acc", [128, 4, T], F32, st)
            sgt = sb("mg_sg", [128, 2, T], F32, st)
            tmp = sb("mg_tmp", [128, 2, T], F32, st)
            b_yT, b_acc = B("yT"), [B(f"acc{i}") for i in range(4)]
            b_sgt, b_tmp = [B("sgt0"), B("sgt1")], [B("tmp0"), B("tmp1")]
            gp, b_gp = load_gpost(1, st)
            alloc_ystage(st)
            n = 0
            for half in range(2):
                for b in range(3):
                    wg, wgB = w_get(31 + 2 * b + half)
                    wr, wrB = w_get(37 + b, ahead=1)
                    wg3 = wg.rearrange("p (c n) -> p c n", c=8)
                    wr3 = wr.rearrange("p (k n) -> p k n", k=4)
                    for o4 in range(4):
                        oc = half * 4 + o4
                        i2 = n % 2
                        n += 1
                        pg, pz = ps_next(), ps_next()
                        for dc in range(8):
                            c.mm(PS[:, pg, :], wg3[:, dc, o4 * 128:(o4 + 1) * 128], xT[:, dc, :], [wgB, b_xT], [b_ps[pg]],
                                 start=(dc == 0), stop=(dc == 7))
                        for kc in range(4):
                            c.mm(PS[:, pz, :], wr3[:, kc, oc * 128:(oc + 1) * 128], ybr_box[b][:, kc, :], [wrB, b_ybr[b]], [b_ps[pz]],
                                 start=(kc == 0), stop=(kc == 3))
                        bcol = 48 + b * 8 + oc
                        c.act(sgt[:, i2, :], PS[:, pg, :], AF.Sigmoid, [b_ps[pg], b_const], [b_sgt[i2]], bias=gcols[:, bcol:bcol + 1])
                        if b == 0:
                            c.tt(acc[:, o4, :], sgt[:, i2, :], PS[:, pz, :], ALU.mult, [b_sgt[i2], b_ps[pz]], [b_acc[o4]])
                        else:
                            c.tt(tmp[:, i2, :], sgt[:, i2, :], PS[:, pz, :], ALU.mult, [b_sgt[i2], b_ps[pz]], [b_tmp[i2]])
                            if b == 1:
                                c.tt(acc[:, o4, :], acc[:, o4, :], tmp[:, i2, :], ALU.add, [b_acc[o4], b_tmp[i2]], [b_acc[o4]])
                            else:
                                c.tt(yT[:, oc, :], acc[:, o4, :], tmp[:, i2, :], ALU.add, [b_acc[o4], b_tmp[i2]], [b_yT])
                    w_done()
                    w_done()
            preload_sqrt_table()
            for ch in range(2):
                w, wB = w_get(40 + ch)
                w3 = w.rearrange("p (c n) -> p c n", c=8)
                bk = (4, 5, 6, 7) if ch == 0 else (0, 1, 2, 3)
                for tc in range(4):
                    for kc in range(8):
                        c.mm(PS[:, bk[tc], :], yT[:, kc, tc * 128:(tc + 1) * 128], w3[:, kc, :], [b_yT, wB], [b_ps[bk[tc]]],
                             start=(kc == 0), stop=(kc == 7))
                w_done()
                for tc in range(4):
                    evac_y(tc, ch, bk[tc], gp, b_gp)
            c.mark("tail")
            sublayer_tail(1.0, {"gcol": G_FFN2})

        def mixer(seq, ti):
            with ExitStack() as mst:
                ybr_box[0] = sb("ybr_hg", [128, 4, T], BF16, mst)
                ybr_box[1] = sb("ybr_dil", [128, 4, T], BF16, mst)
                ybr_box[2] = sb("ybr_mem", [128, 4, T], BF16, mst)
                with ExitStack() as qst:
                    actx = {}
                    actx["qg"] = [sb(f"at_q{g}", [128, 4, T], BF16, qst) for g in range(3)]
                    actx["mq"] = sb("at_mq", [128, 4, T], BF16, qst)
                    pgen = attn_proj_gen(ti, actx)
                    with ExitStack() as st:
                        hgrn(ti, st, pgen)
                    c.barrier(BDS)
                    c.mark(f"t{ti}.attn")
                    with ExitStack() as st:
                        attn(ti, st, actx)
                    c.barrier(BDS)
                    c.mark(f"t{ti}.merge")
                if ti == build_program.dump_tile:
                    for bi, nm in enumerate(("yhg", "ydil", "ymem")):
                        dump(nm, ybr_box[bi][:, :, :], [128, 4, T], [b_ybr[bi]], BF16)
                with ExitStack() as st:
                    merge(st)

        for seq in range(nseq):
            seq_prologue(seq)
            first_prenorm(seq)
            for ti in range(nt):
                tile(seq, ti)
        for d in d_o:
            SP.e.wait_ge(d.sem, d.count)
        if d_dbg.count:
            SP.e.wait_ge(d_dbg.sem, d_dbg.count)
        build_program.n_ins = c.n_ins
        build_program.marks = c.marks
    return nc, dbg_out


build_program.stage = 3
build_program.skipB = False
build_program.dump_tile = -1


def _blk_cols(w, cols):
    sub = w[:, cols]
    return np.ascontiguousarray(sub.reshape(8, 128, sub.shape[1]).transpose(1, 0, 2)).reshape(128, -1)


def _prep_weights(inp):
    blocks = []

    def ffn_blocks(wgu, wdn):
        out = []
        for b in range(11):
            cols = []
            for j in (2 * b, 2 * b + 1):
                cols += list(range(j * 128, (j + 1) * 128)) + list(range(2816 + j * 128, 2816 + (j + 1) * 128))
            out.append(_blk_cols(wgu, np.array(cols)))
        for ch in range(2):
            for b in range(3):
                blk = np.zeros((128, 8, 512), np.float32)
                nf = 8 if b < 2 else 6
                for fi in range(nf):
                    f = b * 8 + fi
                    blk[:, fi, :] = wdn[f * 128:(f + 1) * 128, ch * 512:(ch + 1) * 512]
                out.append(blk.reshape(128, -1))
        return out

    blocks += ffn_blocks(inp["ffn1_w_gu"][0], inp["ffn1_w_down"][0])
    w_in = inp["w_in"][0]
    for b in range(14):
        blocks.append(_blk_cols(w_in, np.arange(b * 512, (b + 1) * 512)))
    for b in range(6):
        blocks.append(_blk_cols(w_in, np.arange(7168 + b * 512, 7168 + (b + 1) * 512)))
    for name in ("w_br_hg", "w_br_dil", "w_br_mem"):
        w = inp[name][0]
        blocks.append(np.ascontiguousarray(w.reshape(4, 128, 1024).transpose(1, 0, 2)).reshape(128, -1))
    w_out = inp["w_out"][0]
    for ch in range(2):
        blocks.append(_blk_cols(w_out, np.arange(ch * 512, (ch + 1) * 512)))
    blocks += ffn_blocks(inp["ffn2_w_gu"][0], inp["ffn2_w_down"][0])
    wm = inp["w_mem_kv"][0]
    for ch in range(2):
        blocks.append(_blk_cols(wm, np.arange(ch * 512, (ch + 1) * 512)))
    assert len(blocks) == NBLK
    return np.stack(blocks).astype(np.float32)


def _prep_gcols(inp):
    g = np.zeros((128, 96), np.float32)
    fm = lambda v: np.asarray(v, np.float32).reshape(8, 128).T
    g[:, 0:8] = fm(inp["ffn1_pre_w"][0])
    g[:, 8:16] = fm(inp["mix_pre_w"][0])
    g[:, 16:24] = fm(inp["ffn2_pre_w"][0])
    g[:, 24:32] = fm(inp["mem_norm_w"][0])
    g[:, 32] = np.asarray(inp["hg_norm_w"][0], np.float32)
    lbl = np.asarray(inp["hg_lb_logits"], np.float32)
    g[:, 40:44] = lbl[0].reshape(4, 128).T
    g[:, 44:48] = lbl[1].reshape(4, 128).T
    g[:, 48:72] = np.asarray(inp["b_gate"][0], np.float32).reshape(24, 128).T
    return g


def _prep_consts():
    NM = 4 * NSLOT_AT * 128
    cm = np.zeros((128, NM + 512 + 512 + 256), np.float32)
    am = np.zeros((128, 4, NSLOT_AT, 128), np.float32)
    kj = np.arange(128)[:, None].astype(np.float64)
    qi = np.arange(128)[None, :].astype(np.float64)
    for hh in range(4):
        sl = [2.0 ** (-8.0 * (g * 4 + hh + 1) / 12.0) for g in range(3)]
        d = qi - kj
        am[:, hh, 6] = np.where(d >= 0, np.exp(-sl[0] * d), 0.0)
        d = 128 + qi - kj
        am[:, hh, 5] = np.where(d <= 128, np.exp(-sl[0] * d), 0.0)
        d = qi - kj
        am[:, hh, 7] = np.where(d >= 0, np.exp(-sl[1] * 4 * d), 0.0)
        d = 128 + qi - kj
        am[:, hh, 4] = np.where(d <= 128, np.exp(-sl[1] * 4 * d), 0.0)
        same = (np.arange(128)[:, None] % 4) == (np.arange(128)[None, :] % 4)
        aq = np.floor(qi / 4)
        ak = np.floor(kj / 4)
        for dist, slot in ((4, 0), (3, 1), (2, 2), (1, 3), (0, 8)):
            d = 32 * dist + aq - ak
            ok = same & (d >= 0) & (d <= 128)
            am[:, hh, slot] = np.where(ok, np.exp(-sl[2] * 16 * d), 0.0)
    cm[:, 0:NM] = am.reshape(128, -1)
    s = np.arange(128)[:, None]
    cc = np.arange(128)[None, :]
    hm = ((s // 64) == (cc // 64)) & (s <= cc)
    cm[:, NM:NM + 512] = np.tile(hm.astype(np.float32), (1, 4))
    rm = np.ones((128, 512), np.float32)
    rm[:, 0::64] = 0.0
    cm[:, NM + 512:NM + 1024] = rm
    cm[:, NM + 1024:NM + 1152] = np.eye(128, dtype=np.float32)
    cm[:, NM + 1152:NM + 1280] = 1.0
    return cm


def make_in_maps(inp, ncores, nseq, nt):
    S = nt * T
    wall = _prep_weights(inp)
    gcols = _prep_gcols(inp)
    gpost = np.stack([np.broadcast_to(np.asarray(inp[n][0], np.float32)[None, :], (128, D)).copy()
                      for n in ("ffn1_post_w", "mix_post_w", "ffn2_post_w")])
    cmask = _prep_consts()
    x = np.asarray(inp["x"], np.float32)
    mem = np.asarray(inp["mem"], np.float32)
    maps = []
    for i in range(ncores):
        maps.append({
            "x": np.ascontiguousarray(x[i * nseq:(i + 1) * nseq, :S]),
            "mem": np.ascontiguousarray(mem[i * nseq:(i + 1) * nseq]),
            "wall": wall, "gcols": gcols, "gpost": gpost, "cmask": cmask,
        })
    return maps


def kernel(**inputs):
    inp = {k: np.asarray(v) for k, v in inputs.items()}
    ncores, nseq, nt = 8, 2, 8
    nc, _ = build_program(nseq, nt)
    maps = make_in_maps(inp, ncores, nseq, nt)
    res = run_bass_kernel_spmd(nc, maps, core_ids=list(range(ncores)))
    out = np.concatenate([np.asarray(r["out"]) for r in res.results], axis=0)
    return out.astype(np.float32)
```
